# Optimizing a Trainium2 kernel written in Bass

```python
import jax, jax.numpy as jnp
from jax import lax
import numpy as np

D_MODEL = 1024
BATCH = 1
SEQ = 16384
DEPTH = 4
DEC_BATCH = 8
DEC_SEQ = 16
PAST_LEN = 1024

CHUNK = 64
HEAD_DIM = 64
A_HEADS = 8
A_KV_HEADS = 2
A_GROUP = A_HEADS // A_KV_HEADS
A_WINDOW = 128
A_BACK = A_WINDOW // CHUNK
B_HEADS = 8
B_BACK = 8
REL_CLIP = 256
X_HEADS = 4
X_HEAD_DIM = 128
N_MEM = 256
D_FF = 2816
CONV_W = 3
EPS = 1e-6
NEG = -1e30
A_Q = A_HEADS * HEAD_DIM
A_KV = A_KV_HEADS * HEAD_DIM
B_W = B_HEADS * HEAD_DIM
X_W = X_HEADS * X_HEAD_DIM
D_IN = A_Q + 2 * A_KV + 3 * B_W + 2 * D_MODEL

kernel_name = 'hybrid_streaming_swa_chunkband_step'


def rmsnorm(x, g):
    xf = x.astype(jnp.float32)
    y = xf * lax.rsqrt(jnp.mean(xf * xf, axis=-1, keepdims=True) + EPS)
    return (y * g.astype(jnp.float32)).astype(x.dtype)


def alibi_slopes():
    return jnp.asarray(np.exp2(-8.0 * np.arange(1, A_HEADS + 1) / A_HEADS), dtype=jnp.float32)


def alibi_bias(slopes, pos_q, pos_k):
    dist = jnp.abs(pos_q[:, None] - pos_k[None, :]).astype(jnp.float32)
    return -slopes.reshape(A_KV_HEADS, A_GROUP, 1, 1) * dist


def rel_pos_bias(table, pos_q, pos_k):
    idx = jnp.clip(pos_k[None, :] - pos_q[:, None], -REL_CLIP, REL_CLIP) + REL_CLIP
    return table.astype(jnp.float32)[:, idx][:, None]


def band_positions(back):
    pos_q = back * CHUNK + jnp.arange(CHUNK)
    pos_k = jnp.arange((back + 1) * CHUNK)
    return pos_q, pos_k


def band_valid_add(nc, back):
    kc = jnp.arange(nc)[:, None] - back + jnp.arange((back + 1) * CHUNK)[None, :] // CHUNK
    return jnp.where(kc >= 0, 0.0, NEG)[:, None, None, None, :]


def band_mask_add(pos_q, pos_k, back):
    cq = pos_q[:, None] // CHUNK
    ck = pos_k[None, :] // CHUNK
    return jnp.where((ck <= cq) & (ck >= cq - back), 0.0, NEG)


def chunk_band(t, back):
    nc = t.shape[1]
    pad = jnp.pad(t, [(0, 0), (back, 0)] + [(0, 0)] * (t.ndim - 2))
    band = jnp.stack([pad[:, j:j + nc] for j in range(back + 1)], axis=2)
    return band.reshape(t.shape[:2] + ((back + 1) * t.shape[2],) + t.shape[3:])


def attend(q, k, v, bias, mask_add, sink=None):
    s = jnp.einsum('...qkgd,...skd->...kgqs', q.astype(jnp.float32), k.astype(jnp.float32)) * (q.shape[-1] ** -0.5)
    if bias is not None:
        s = s + bias
    if mask_add is not None:
        s = s + mask_add
    if sink is not None:
        sk = jnp.broadcast_to(sink.astype(jnp.float32)[:, :, None, None], s.shape[:-1] + (1,))
        p = jax.nn.softmax(jnp.concatenate([s, sk], axis=-1), axis=-1)[..., :-1]
    else:
        p = jax.nn.softmax(s, axis=-1)
    return jnp.einsum('...kgqs,...skd->...qkgd', p.astype(v.dtype), v)


def project_in(h, w_in):
    z = h @ w_in
    c0 = A_Q
    c1 = c0 + A_KV
    c2 = c1 + A_KV
    c3 = c2 + B_W
    c4 = c3 + B_W
    c5 = c4 + B_W
    c6 = c5 + D_MODEL
    return jnp.split(z, [c0, c1, c2, c3, c4, c5, c6], axis=-1)


def merge_branches(oa, ob, ga, gb, w_o_a, w_o_b, w_out):
    m = jax.nn.sigmoid(ga) * (oa @ w_o_a) + jax.nn.sigmoid(gb) * (ob @ w_o_b)
    return m @ w_out


def mixer_prompt(h, slopes, w_in, sink, rel_table, w_o_a, w_o_b, w_out):
    bp, t, _ = h.shape
    nc = t // CHUNK
    qa, ka, va, qb, kb, vb, ga, gb = project_in(h, w_in)
    ka = ka.reshape(bp, t, A_KV_HEADS, HEAD_DIM)
    va = va.reshape(bp, t, A_KV_HEADS, HEAD_DIM)
    kb = kb.reshape(bp, t, B_HEADS, HEAD_DIM)
    vb = vb.reshape(bp, t, B_HEADS, HEAD_DIM)
    qa_c = qa.reshape(bp, nc, CHUNK, A_KV_HEADS, A_GROUP, HEAD_DIM)
    ka_b = chunk_band(ka.reshape(bp, nc, CHUNK, A_KV_HEADS, HEAD_DIM), A_BACK)
    va_b = chunk_band(va.reshape(bp, nc, CHUNK, A_KV_HEADS, HEAD_DIM), A_BACK)
    pq, pk = band_positions(A_BACK)
    oa = attend(qa_c, ka_b, va_b, alibi_bias(slopes, pq, pk), band_valid_add(nc, A_BACK),
                sink.reshape(A_KV_HEADS, A_GROUP)).reshape(bp, t, A_Q)
    qb_c = qb.reshape(bp, nc, CHUNK, B_HEADS, 1, HEAD_DIM)
    kb_b = chunk_band(kb.reshape(bp, nc, CHUNK, B_HEADS, HEAD_DIM), B_BACK)
    vb_b = chunk_band(vb.reshape(bp, nc, CHUNK, B_HEADS, HEAD_DIM), B_BACK)
    pq, pk = band_positions(B_BACK)
    ob = attend(qb_c, kb_b, vb_b, rel_pos_bias(rel_table, pq, pk), band_valid_add(nc, B_BACK)).reshape(bp, t, B_W)
    y = merge_branches(oa, ob, ga, gb, w_o_a, w_o_b, w_out)
    na = min(A_WINDOW, t)
    nb = min(B_BACK * CHUNK, t)
    return y, ka[:, t - na:], va[:, t - na:], kb[:, t - nb:], vb[:, t - nb:]


def mixer_sample(h, cak, cav, cbk, cbv, slopes, w_in, sink, rel_table, w_o_a, w_o_b, w_out):
    bd, tn, _ = h.shape
    qa, ka, va, qb, kb, vb, ga, gb = project_in(h, w_in)
    ka = ka.reshape(bd, tn, A_KV_HEADS, HEAD_DIM)
    va = va.reshape(bd, tn, A_KV_HEADS, HEAD_DIM)
    kb = kb.reshape(bd, tn, B_HEADS, HEAD_DIM)
    vb = vb.reshape(bd, tn, B_HEADS, HEAD_DIM)
    pos_q = PAST_LEN + jnp.arange(tn)
    la = cak.shape[1]
    pk_a = jnp.concatenate([PAST_LEN - la + jnp.arange(la), pos_q])
    oa = attend(qa.reshape(bd, tn, A_KV_HEADS, A_GROUP, HEAD_DIM),
                jnp.concatenate([cak.astype(ka.dtype), ka], axis=1),
                jnp.concatenate([cav.astype(va.dtype), va], axis=1),
                alibi_bias(slopes, pos_q, pk_a), band_mask_add(pos_q, pk_a, A_BACK),
                sink.reshape(A_KV_HEADS, A_GROUP)).reshape(bd, tn, A_Q)
    lb = cbk.shape[1]
    pk_b = jnp.concatenate([PAST_LEN - lb + jnp.arange(lb), pos_q])
    ob = attend(qb.reshape(bd, tn, B_HEADS, 1, HEAD_DIM),
                jnp.concatenate([cbk.astype(kb.dtype), kb], axis=1),
                jnp.concatenate([cbv.astype(vb.dtype), vb], axis=1),
                rel_pos_bias(rel_table, pos_q, pk_b), band_mask_add(pos_q, pk_b, B_BACK)).reshape(bd, tn, B_W)
    y = merge_branches(oa, ob, ga, gb, w_o_a, w_o_b, w_out)
    return y, ka, va, kb, vb


def mem_kv(mem, g, w_k, w_v):
    b, m, _ = mem.shape
    hm = rmsnorm(mem, g)
    return ((hm @ w_k).reshape(b, m, X_HEADS, X_HEAD_DIM), (hm @ w_v).reshape(b, m, X_HEADS, X_HEAD_DIM))


def cross_attn(h, mk, mv, w_q, w_o):
    b, t, _ = h.shape
    q = (h @ w_q).reshape(b, t, X_HEADS, 1, X_HEAD_DIM)
    o = attend(q, mk.astype(h.dtype), mv.astype(h.dtype), None, None)
    return o.reshape(b, t, X_W) @ w_o


def conv_ffn(h, conv_state, w_up, w_conv, b_conv, w_down):
    u = h @ w_up
    t = u.shape[1]
    ext = jnp.concatenate([conv_state.astype(u.dtype), u], axis=1)
    c = b_conv
    for j in range(CONV_W):
        c = c + ext[:, j:j + t] * w_conv[j]
    gate, up = jnp.split(c, 2, axis=-1)
    return (jax.nn.silu(gate) * up) @ w_down, ext[:, t:]


def setup_inputs(seed: int = 0) -> dict:
    key = jax.random.key(seed)
    ks = jax.random.split(key, 32)

    def nrm(k, shape, scale=1.0):
        return scale * jax.random.normal(k, shape, jnp.float32)

    la = min(A_WINDOW, PAST_LEN)
    lb = min(B_BACK * CHUNK, PAST_LEN)
    return {
        'x_prompt': nrm(ks[0], (BATCH, SEQ, D_MODEL)),
        'x_sample': nrm(ks[1], (DEC_BATCH, DEC_SEQ, D_MODEL)),
        'mem_prompt': nrm(ks[2], (BATCH, N_MEM, D_MODEL)),
        'cache_a_k': nrm(ks[3], (DEPTH, DEC_BATCH, la, A_KV_HEADS, HEAD_DIM)),
        'cache_a_v': nrm(ks[4], (DEPTH, DEC_BATCH, la, A_KV_HEADS, HEAD_DIM)),
        'cache_b_k': nrm(ks[5], (DEPTH, DEC_BATCH, lb, B_HEADS, HEAD_DIM)),
        'cache_b_v': nrm(ks[6], (DEPTH, DEC_BATCH, lb, B_HEADS, HEAD_DIM)),
        'cache_mem_k': nrm(ks[7], (DEPTH, DEC_BATCH, N_MEM, X_HEADS, X_HEAD_DIM)),
        'cache_mem_v': nrm(ks[8], (DEPTH, DEC_BATCH, N_MEM, X_HEADS, X_HEAD_DIM)),
        'state_conv': nrm(ks[9], (DEPTH, DEC_BATCH, CONV_W - 1, 2 * D_FF)),
        'g_mix': 1.0 + nrm(ks[10], (DEPTH, D_MODEL), 0.02),
        'w_mix_in': nrm(ks[11], (DEPTH, D_MODEL, D_IN), D_MODEL ** -0.5),
        'a_sink': nrm(ks[12], (DEPTH, A_HEADS), 0.5),
        'b_rel_bias': nrm(ks[13], (DEPTH, B_HEADS, 2 * REL_CLIP + 1), 0.1),
        'w_o_a': nrm(ks[14], (DEPTH, A_Q, D_MODEL), A_Q ** -0.5),
        'w_o_b': nrm(ks[15], (DEPTH, B_W, D_MODEL), B_W ** -0.5),
        'w_mix_out': nrm(ks[16], (DEPTH, D_MODEL, D_MODEL), D_MODEL ** -0.5),
        'g_xattn': 1.0 + nrm(ks[17], (DEPTH, D_MODEL), 0.02),
        'g_mem': 1.0 + nrm(ks[18], (DEPTH, D_MODEL), 0.02),
        'w_xq': nrm(ks[19], (DEPTH, D_MODEL, X_W), D_MODEL ** -0.5),
        'w_xk': nrm(ks[20], (DEPTH, D_MODEL, X_W), D_MODEL ** -0.5),
        'w_xv': nrm(ks[21], (DEPTH, D_MODEL, X_W), D_MODEL ** -0.5),
        'w_xo': nrm(ks[22], (DEPTH, X_W, D_MODEL), X_W ** -0.5),
        'g_ffn': 1.0 + nrm(ks[23], (DEPTH, D_MODEL), 0.02),
        'w_up': nrm(ks[24], (DEPTH, D_MODEL, 2 * D_FF), D_MODEL ** -0.5),
        'w_conv': nrm(ks[25], (DEPTH, CONV_W, 2 * D_FF), CONV_W ** -0.5),
        'b_conv': nrm(ks[26], (DEPTH, 2 * D_FF), 0.01),
        'w_down': nrm(ks[27], (DEPTH, D_FF, D_MODEL), D_FF ** -0.5),
        'g_final': 1.0 + nrm(ks[28], (D_MODEL,), 0.02),
    }


def reference(x_prompt, x_sample, mem_prompt, cache_a_k, cache_a_v, cache_b_k, cache_b_v,
              cache_mem_k, cache_mem_v, state_conv, g_mix, w_mix_in, a_sink, b_rel_bias,
              w_o_a, w_o_b, w_mix_out, g_xattn, g_mem, w_xq, w_xk, w_xv, w_xo,
              g_ffn, w_up, w_conv, b_conv, w_down, g_final):
    slopes = alibi_slopes()
    xp = x_prompt
    xs = x_sample
    l_pa_k, l_pa_v, l_pb_k, l_pb_v, l_pm_k, l_pm_v, l_pconv = [], [], [], [], [], [], []
    l_sa_k, l_sa_v, l_sb_k, l_sb_v, l_sconv = [], [], [], [], []
    for l in range(DEPTH):
        wts = (w_mix_in[l], a_sink[l], b_rel_bias[l], w_o_a[l], w_o_b[l], w_mix_out[l])
        y, ak, av, bk, bv = mixer_prompt(rmsnorm(xp, g_mix[l]), slopes, *wts)
        xp = xp + y
        mk, mv = mem_kv(mem_prompt, g_mem[l], w_xk[l], w_xv[l])
        xp = xp + cross_attn(rmsnorm(xp, g_xattn[l]), mk, mv, w_xq[l], w_xo[l])
        zero_state = jnp.zeros((xp.shape[0], CONV_W - 1, 2 * D_FF), xp.dtype)
        y, cp = conv_ffn(rmsnorm(xp, g_ffn[l]), zero_state, w_up[l], w_conv[l], b_conv[l], w_down[l])
        xp = xp + y
        l_pa_k.append(ak); l_pa_v.append(av); l_pb_k.append(bk); l_pb_v.append(bv)
        l_pm_k.append(mk); l_pm_v.append(mv); l_pconv.append(cp)
        y, ak, av, bk, bv = mixer_sample(rmsnorm(xs, g_mix[l]), cache_a_k[l], cache_a_v[l],
                                         cache_b_k[l], cache_b_v[l], slopes, *wts)
        xs = xs + y
        xs = xs + cross_attn(rmsnorm(xs, g_xattn[l]), cache_mem_k[l], cache_mem_v[l], w_xq[l], w_xo[l])
        y, cs = conv_ffn(rmsnorm(xs, g_ffn[l]), state_conv[l], w_up[l], w_conv[l], b_conv[l], w_down[l])
        xs = xs + y
        l_sa_k.append(ak); l_sa_v.append(av); l_sb_k.append(bk); l_sb_v.append(bv); l_sconv.append(cs)
    y_prompt = rmsnorm(xp, g_final)
    y_sample = rmsnorm(xs, g_final)
    pa_k = jnp.stack(l_pa_k)
    pa_v = jnp.stack(l_pa_v)
    pb_k = jnp.stack(l_pb_k)
    pb_v = jnp.stack(l_pb_v)
    pm_k = jnp.stack(l_pm_k)
    pm_v = jnp.stack(l_pm_v)
    pconv = jnp.stack(l_pconv)
    sa_k = jnp.stack(l_sa_k)
    sa_v = jnp.stack(l_sa_v)
    sb_k = jnp.stack(l_sb_k)
    sb_v = jnp.stack(l_sb_v)
    sconv = jnp.stack(l_sconv)
    return (y_prompt, y_sample, pa_k, pa_v, pb_k, pb_v, pm_k, pm_v, pconv, sa_k, sa_v, sb_k, sb_v, sconv)
```

```python
import contextlib
import numpy as np
import concourse.bass as bass
import concourse.mybir as mybir
from concourse.bass_utils import run_bass_kernel_spmd

F32 = mybir.dt.float32
BF16 = mybir.dt.bfloat16
ALU = mybir.AluOpType
AF = mybir.ActivationFunctionType

DEPTH = 4
D = 1024
NBLK = 36
FB = [4, 9, 14, 19]
LTOK = NBLK * 128
HALO = 2560
DFF = 2816
NEGM = -30000.0
EPS = 1e-6
NVON = 20
COMPUTE = ('pe', 'act', 'dve', 'pool')


class Sched:
    def __init__(self):
        self.ops = []
        self.lastw = {}
        self.rd = {}

    def op(self, eng, fn, reads=(), writes=(), dma=False):
        i = len(self.ops)
        deps = set()
        writes = list(writes) + [k for k in reads if isinstance(k, tuple) and k[0] == 'pb']
        for k in reads:
            w = self.lastw.get(k)
            if w is not None:
                deps.add(w)
        for k in writes:
            w = self.lastw.get(k)
            if w is not None:
                deps.add(w)
            r = self.rd.get(k)
            if r:
                deps.update(r[0].values())
                deps.update(r[1])
        self.ops.append({'eng': eng, 'fn': fn, 'dma': dma, 'deps': deps, 'need': False})
        for k in reads:
            r = self.rd.setdefault(k, ({}, []))
            if dma:
                r[1].append(i)
            else:
                r[0][eng] = i
        for k in writes:
            self.lastw[k] = i
            self.rd[k] = ({}, [])
        return i

    def finalize(self, rings):
        ops = self.ops
        for o in ops:
            keep = set()
            for d in o['deps']:
                p = ops[d]
                if p['eng'] == 'pe' and o['eng'] == 'pe' and not p['dma'] and not o['dma']:
                    continue
                p['need'] = True
                keep.add(d)
            o['deps'] = keep
        cnt = {e: 0 for e in COMPUTE}
        dcount = {q: 0 for q in rings}
        for o in ops:
            if o['dma']:
                q = o['eng']
                k = dcount[q]
                dcount[q] += 1
                R = len(rings[q])
                o['sem'] = ('ring', q, k % R)
                o['val'] = 16 * (k // R + 1)
                o['gate'] = 16 * (k // R)
            elif o['need']:
                cnt[o['eng']] += 1
                o['sem'] = ('eng', o['eng'])
                o['val'] = cnt[o['eng']]
        self.final_ring = {}
        for o in ops:
            if o['dma']:
                self.final_ring[o['sem']] = max(self.final_ring.get(o['sem'], 0), o['val'])

    def emit_engine(self, engname, eng, semh):
        ops = self.ops
        waited = {}

        def wait(sem, val):
            if val > waited.get(sem, 0):
                eng.wait_ge(semh[sem], val)
                waited[sem] = val

        for o in ops:
            if o['eng'] != engname:
                continue
            need = {}
            for d in o['deps']:
                p = ops[d]
                need[p['sem']] = max(need.get(p['sem'], 0), p['val'])
            if o['dma'] and o['gate'] > 0:
                need[o['sem']] = max(need.get(o['sem'], 0), o['gate'])
            for s_, v_ in need.items():
                wait(s_, v_)
            ins = o['fn'](eng)
            if o['dma']:
                ins.then_inc(semh[o['sem']], 16)
            elif o['need']:
                ins.then_inc(semh[o['sem']], 1)
        for s_, v_ in self.final_ring.items():
            if s_[1] == engname:
                wait(s_, v_)


def build(dbg=None):
    nc = bass.Bass("TRN2", target_bir_lowering=False)
    S = Sched()

    def din(name, shape, dt=F32):
        return nc.dram_tensor(name, list(shape), dt, kind="ExternalInput")

    def dout(name, shape, dt=F32):
        return nc.dram_tensor(name, list(shape), dt, kind="ExternalOutput")

    def dscr(name, shape, dt=BF16):
        return nc.dram_tensor(name, list(shape), dt)

    xin = din("xin", [LTOK, D])
    xs_in = din("xs", [16, D])
    mem_in = din("mem", [256, D])
    cak = din("cak", [DEPTH, 128, 128]); cav = din("cav", [DEPTH, 128, 128])
    cbk = din("cbk", [DEPTH, 512, 512]); cbv = din("cbv", [DEPTH, 512, 512])
    cmk = din("cmk", [DEPTH, 256, 512]); cmv = din("cmv", [DEPTH, 256, 512])
    sconv_in = din("sconv_in", [DEPTH, 128, 44, 2])
    g_mix = din("g_mix", [DEPTH, D]); g_xattn = din("g_xattn", [DEPTH, D]); g_mem = din("g_mem", [DEPTH, D])
    g_ffn = din("g_ffn", [DEPTH, D]); g_final = din("g_final", [1, D])
    w_mix_in = din("w_mix_in", [DEPTH, D, 4352])
    sink_l = din("sink_l", [128, 16])
    rel = din("b_rel_bias", [DEPTH, 8, 513])
    rel_rev = din("rel_rev", [DEPTH, 8, 513])
    w_o_a = din("w_o_a", [DEPTH, 512, D]); w_o_b = din("w_o_b", [DEPTH, 512, D]); w_mix_out = din("w_mix_out", [DEPTH, D, D])
    w_xq = din("w_xq", [DEPTH, D, 512]); w_xk = din("w_xk", [DEPTH, D, 512]); w_xv = din("w_xv", [DEPTH, D, 512])
    w_xo = din("w_xo", [DEPTH, 512, D])
    w_up = din("w_up", [DEPTH, D, 2 * DFF]); w_down = din("w_down", [DEPTH, DFF, D])
    wconv_l = din("wconv_l", [128, DEPTH * 3 * 44]); bconv_l = din("bconv_l", [128, DEPTH * 44])
    c_I = din("c_I", [128, 128]); c_J = din("c_J", [128, 128]); c_J16 = din("c_J16", [16, 16])
    c_A = din("c_A", [128, 2 * 8 * 128]); c_As = din("c_As", [128, 2 * 8 * 16]); c_MB = din("c_MB", [128, 2 * 128])
    c_vblk = din("c_vblk", [128, NBLK])
    c_negrow = din("c_negrow", [1, NVON * 128])

    o_y = dout("o_y", [2048, D]); o_ys = dout("o_ys", [16, D])
    o_pkv = dout("o_pkv", [DEPTH, 4, 128, 1280]); o_skv = dout("o_skv", [DEPTH, 16, 1280])
    o_pm = dout("o_pm", [DEPTH, 256, 1024])
    o_pconv = dout("o_pconv", [DEPTH, 128, 44, 2]); o_sconv = dout("o_sconv", [DEPTH, 128, 44, 2])

    xsc = dscr("xsc", [LTOK, D], F32); xssc = dscr("xssc", [16, D], F32)
    ext = dscr("ext", [DEPTH, 8, 768], F32)
    rext = dscr("rext", [DEPTH, 8, 768], F32)
    s_wq = dscr("s_wq", [DEPTH, D, 1024]); s_wk = dscr("s_wk", [DEPTH, D, 768]); s_wg = dscr("s_wg", [DEPTH, D, 2048])
    s_wv = dscr("s_wv", [DEPTH, D, 768]); s_wkvo = dscr("s_wkvo", [DEPTH, D, 1280])
    s_woa = dscr("s_woa", [DEPTH, 512, D]); s_wob = dscr("s_wob", [DEPTH, 512, D]); s_wout = dscr("s_wout", [DEPTH, D, D])
    s_wxq = dscr("s_wxq", [DEPTH, D, 512]); s_wxkv = dscr("s_wxkv", [DEPTH, D, 1024]); s_wxo = dscr("s_wxo", [DEPTH, 512, D])
    s_wup = dscr("s_wup", [DEPTH, D, 2 * DFF]); s_wdn = dscr("s_wdn", [DEPTH, DFF, D])

    es = contextlib.ExitStack()

    def sb(name, shape, dt):
        return es.enter_context(nc.sbuf_tensor(name, list(shape), dt))

    def pst_(name, shape, dt):
        return es.enter_context(nc.psum_tensor(name, list(shape), dt))

    with es:
        xts = [sb("xt%d" % i, [128, 4, D], F32) for i in range(2)]
        XB = {'i': 0}
        CUR = {'xb': 0}
        g_bc = sb("g_bc", [128, 3, D], F32)
        hn = sb("hn", [128, 4, D], BF16)
        hT = sb("hT", [128, 8, 512], BF16)
        big = sb("big", [128, 12288], BF16)
        kTb = sb("kTb", [128, 4, 1024], BF16); kTa = sb("kTa", [128, 2, 1024], BF16)
        vb = sb("vb", [128, 8, 512], BF16); va = sb("va", [128, 8, 256], BF16)
        PT = [sb("PT%d" % i, [128, 5, 256], BF16) for i in range(2)]
        PTf = [PT[i][:, :, :].rearrange("p a b -> p (a b)") for i in range(2)]
        PTx0 = sb("PTx0", [128, 2, 512], BF16)
        PTx = [PTx0, PTx0]
        mkT = sb("mkT", [128, 4, 256], BF16); mv = sb("mv", [128, 2, 512], BF16)
        hank = sb("hank", [128, 8, 640], BF16)
        cI = sb("cI", [128, 128], BF16); cJ = sb("cJ", [128, 128], BF16); cJ16 = sb("cJ16", [16, 16], BF16)
        ones = sb("ones", [128, 128], BF16); ones32 = sb("ones32", [128, 128], F32)
        cA = sb("cA", [128, 8, 2, 128], BF16); cAs = sb("cAs", [128, 2, 8, 16], BF16); cMB = sb("cMB", [128, 2, 128], BF16)
        negrow = sb("negrow", [1, NVON * 128], BF16)
        onesE = sb("onesE", [128, 128], BF16); onesO = sb("onesO", [128, 128], BF16)
        vblk = sb("vblk", [128, NBLK], F32)
        esink = sb("esink", [128, 16], F32)
        wcv = sb("wcv", [128, DEPTH * 3 * 44], F32); bcv = sb("bcv", [128, DEPTH * 44], F32)
        ugs = [sb("ug%d" % i, [128, 514], F32) for i in range(2)]; uus = [sb("uu%d" % i, [128, 514], F32) for i in range(2)]
        accgs = [sb("accg%d" % i, [128, 512], F32) for i in range(2)]; accus = [sb("accu%d" % i, [128, 512], F32) for i in range(2)]
        ug = ugs[0]; uu = uus[0]; accg = accgs[0]; accu = accus[0]
        uh = sb("uh", [128, 44, 2], F32); ulast = sb("ulast", [128, 44, 2], F32)
        rden = sb("rden", [128, 512], F32)
        ss = sb("ss", [128, 4], F32); rstd = sb("rstd", [128, 4], F32); epsT = sb("epsT", [128, 1], F32)
        stage = sb("stage", [128, 1280], F32)
        cst = stage[:, 0:1024].bitcast(BF16).rearrange("p (a c) -> p a c", c=512)
        wbuf = sb("wbuf", [128, 5, 4096], BF16)
        exb = stage[0:8, 0:512]
        psum = pst_("psum", [128, 4096], F32)

        def bank(i):
            return psum[:, i * 512:(i + 1) * 512]
        psS = [bank(5), bank(6), bank(7)]
        PSK = [('pb', 5), ('pb', 6), ('pb', 7)]

        qT = big[:, 0:4096].rearrange("p (a n) -> p a n", n=512)
        oT = big[:, 4096:8192].rearrange("p (a n) -> p a n", n=512)
        mT = big[:, 8192:12288].rearrange("p (a n) -> p a n", n=512)
        m2T = big[:, 0:11264].rearrange("p (a n) -> p a n", n=512)
        BIG = [('big', r_, t_) for r_ in range(3) for t_ in range(4)]

        def BK(r_, ts):
            return [('big', r_, t_) for t_ in ts]

        def m2key(j):
            return BK((j * 512) // 4096, range(4))

        def MM(out, lhsT, rhs, st, sp_, R, W):
            S.op('pe', lambda e: e.matmul(out, lhsT=lhsT, rhs=rhs, start=st, stop=sp_), R, W)

        def TR(out, in_, ident, R, W):
            S.op('pe', lambda e: e.transpose(out, in_, ident), R, W)

        def ACT(out, in_, func, R, W, bias=None, scale=None, accum=None):
            kw = {}
            if bias is not None:
                kw['bias'] = bias
            if scale is not None:
                kw['scale'] = scale
            if accum is not None:
                kw['accum_out'] = accum
            S.op('act', lambda e: e.activation(out=out, in_=in_, func=func, **kw), R, W)

        def TS(eng, out, in0, s1, s2, op0, op1, R, W):
            if s2 is None:
                S.op(eng, lambda e: e.tensor_scalar(out=out, in0=in0, scalar1=s1, scalar2=None, op0=op0), R, W)
            else:
                S.op(eng, lambda e: e.tensor_scalar(out=out, in0=in0, scalar1=s1, scalar2=s2, op0=op0, op1=op1), R, W)

        def STT(eng, out, in0, sc, in1, op0, op1, R, W):
            S.op(eng, lambda e: e.scalar_tensor_tensor(out=out, in0=in0, scalar=sc, in1=in1, op0=op0, op1=op1), R, W)

        def TT(eng, out, in0, in1, op, R, W):
            S.op(eng, lambda e: e.tensor_tensor(out=out, in0=in0, in1=in1, op=op), R, W)

        def CP(eng, out, in_, R, W):
            if eng == 'act':
                S.op('act', lambda e: e.copy(out=out, in_=in_), R, W)
            else:
                S.op(eng, lambda e: e.tensor_copy(out=out, in_=in_), R, W)

        def SCL(eng, out, in_, factor, R, W):
            if eng == 'act':
                S.op('act', lambda e: e.mul(out=out, in_=in_, mul=factor), R, W)
            else:
                TS(eng, out, in_, factor, None, ALU.mult, None, R, W)

        def DMA(q, out, in_, R, W, slow=False):
            if slow:
                S.op(q, lambda e: e.dma_start(out=out, in_=in_, allow_slow_non_contiguous=True), R, W, dma=True)
            else:
                S.op(q, lambda e: e.dma_start(out=out, in_=in_), R, W, dma=True)

        rot = {'d': 0, 'w': 0, 'tr': 0, 'pt': 0, 'ptx': 0, 'cp': 0}

        def nbank():
            i = rot['d'] % 3
            rot['d'] += 1
            return bank(i), ('pb', i)

        def ntr():
            i = rot['tr'] % 2
            rot['tr'] += 1
            return bank(2 + i).bitcast(BF16), ('pb', 2 + i)

        def cpeng():
            rot['cp'] += 1
            return 'dve' if rot['cp'] % 2 else 'act'

        def wload(pieces, kc, width, krow0=0):
            bi = rot['w'] % 5
            rot['w'] += 1
            view = wbuf[:, bi, 0:kc * width].rearrange("p (k c) -> p k c", c=width)
            for (scr, l, c0, ncols, dcol) in pieces:
                src = scr.ap()[l, krow0 * 128:(krow0 + kc) * 128, c0:c0 + ncols].rearrange("(k p) c -> p k c", p=128)
                DMA('sp', view[:, :, dcol:dcol + ncols], src, PK[(l, scr.name)], [('wbuf', bi)])
            return view, ('wbuf', bi)

        DMA('pool', cI[:], c_I.ap()[:, :], [], ['cI'])
        DMA('pool', cJ[:], c_J.ap()[:, :], [], ['cJ'])
        DMA('pool', cJ16[:], c_J16.ap()[:, :], [], ['cJ16'])
        DMA('pool', cA[:].rearrange("p a b c -> p (a b c)"), c_A.ap()[:, :], [], ['cA'])
        DMA('pool', cAs[:].rearrange("p a b c -> p (a b c)"), c_As.ap()[:, :], [], ['cAs'])
        DMA('pool', cMB[:].rearrange("p a b -> p (a b)"), c_MB.ap()[:, :], [], ['cMB'])
        DMA('sp', vblk[:], c_vblk.ap()[:, :], [], ['vblk'])
        DMA('sp', esink[:], sink_l.ap()[:, :], [], ['esink'])
        DMA('sp', wcv[:], wconv_l.ap()[:, :], [], ['wcv'])
        DMA('sp', bcv[:], bconv_l.ap()[:, :], [], ['bcv'])
        S.op('dve', lambda e: e.memset(ones32[:], 1.0), [], ['ones32'])
        S.op('dve', lambda e: e.memset(epsT[:], EPS), [], ['epsT'])
        CP('dve', ones[:], ones32[:], ['ones32'], ['ones'])
        ACT(esink[:], esink[:], AF.Exp, ['esink'], ['esink'])
        DMA('pool', negrow[:, :], c_negrow.ap()[:, :], [], ['negrow'])
        S.op('dve', lambda e: e.memset(onesE[:, :], 0.0), [], ['onesE'])
        S.op('dve', lambda e: e.memset(onesO[:, :], 0.0), [], ['onesO'])
        CP('dve', onesE[:, 0:64], ones32[:, 0:64], ['ones32', 'onesE'], ['onesE'])
        CP('dve', onesO[:, 64:128], ones32[:, 64:128], ['ones32', 'onesO'], ['onesO'])

        def vones_ap(gb):
            return ones[:], 'ones'

        PK = {}

        def prep_list(l):
            P = []

            def cast(dst, dc0, src, sc0, ncols, nrows):
                for r0 in range(0, nrows, 1024):
                    r1 = min(nrows, r0 + 1024)
                    for c in range(0, ncols, 512):
                        w_ = min(512, ncols - c)
                        key = ('wp', l, dst.name, r0, dc0 + c)
                        PK.setdefault((l, dst.name), []).append(key)
                        P.append((dst.ap()[l, r0:r1, dc0 + c:dc0 + c + w_], src.ap()[l, r0:r1, sc0 + c:sc0 + c + w_], key))
            cast(s_wxkv, 0, w_xk, 0, 512, D); cast(s_wxkv, 512, w_xv, 0, 512, D)
            cast(s_wk, 0, w_mix_in, 1280, 512, D)
            for i in range(4):
                cast(s_wk, 512 + 64 * i, w_mix_in, 512 + 64 * (i // 2), 64, D)
                cast(s_wv, 512 + 64 * i, w_mix_in, 640 + 64 * (i // 2), 64, D)
            cast(s_wv, 0, w_mix_in, 1792, 512, D)
            key = ('wp', l, 'ext')
            PK.setdefault((l, 'ext'), []).append(key)
            P.append((ext.ap()[l, :, 383:767], rel.ap()[l, :, 0:384], key))
            cast(s_wq, 0, w_mix_in, 0, 512, D); cast(s_wq, 512, w_mix_in, 768, 512, D)
            cast(s_wkvo, 0, w_mix_in, 512, 256, D); cast(s_wkvo, 256, w_mix_in, 1280, 1024, D)
            cast(s_wg, 0, w_mix_in, 2304, 2048, D)
            cast(s_woa, 0, w_o_a, 0, D, 512); cast(s_wob, 0, w_o_b, 0, D, 512); cast(s_wout, 0, w_mix_out, 0, D, D)
            cast(s_wxq, 0, w_xq, 0, 512, D)
            cast(s_wxo, 0, w_xo, 0, D, 512)
            cast(s_wup, 0, w_up, 0, 2 * DFF, D); cast(s_wdn, 0, w_down, 0, D, DFF)
            return P

        def do_prep(l):
            for (dst, src, key) in prep_list(l):
                DMA('pool', dst, src, [], [key], slow=True)

        def norm_stats(gi, bs, t):
            ACT(hn[:bs, t, :], xts[CUR['xb']][:bs, t, :], AF.Square, [('xt', CUR['xb'], t)], [('hn', t), ('ss', t)], accum=ss[:bs, t:t + 1])
            ACT(rstd[:bs, t:t + 1], ss[:bs, t:t + 1], AF.Sqrt, [('ss', t), 'epsT'], [('rstd', t)], bias=epsT[:bs, :], scale=1.0 / D)

        def norm_scale(gi, bs, t):
            S.op('dve', (lambda t=t: (lambda e: e.reciprocal(out=rstd[:bs, t:t + 1], in_=rstd[:bs, t:t + 1])))(), [('rstd', t)], [('rstd', t)])
            STT('dve', hn[:bs, t, :], xts[CUR['xb']][:bs, t, :], rstd[:bs, t:t + 1], g_bc[:bs, gi, :], ALU.mult, ALU.mult,
                [('xt', CUR['xb'], t), ('rstd', t), ('g', gi)], [('hn', t)])

        def norm_tr(bs, blks):
            for t in blks:
                pt_, pk = ntr()
                for kc in range(8):
                    TR(pt_[:, kc * 128:kc * 128 + bs], hn[:bs, t, kc * 128:(kc + 1) * 128], cI[:bs, :bs], [('hn', t), 'cI'], [pk])
                CP(cpeng(), hT[:, :, t * 128:t * 128 + bs], pt_[:, :].rearrange("p (k c) -> p k c", c=128)[:, :, 0:bs], [pk], [('hT', t)])

        def norm_to_hT(l_g, gi, nb, bs, blocks=None):
            blks = list(blocks if blocks is not None else range(nb))
            for t in blks:
                norm_stats(gi, bs, t)
            for t in blks:
                norm_scale(gi, bs, t)
            norm_tr(bs, blks)

        def hT_keys(nb):
            return [('hT', t) for t in range(nb)]

        def dense_fm(wview, wkey, kc_n, col0, rhs_fn, rkeys, n, evac):
            pb, pk = nbank()
            for kc in range(kc_n):
                MM(pb[:, 0:n], wview[:, kc, col0:col0 + 128], rhs_fn(kc), kc == 0, kc == kc_n - 1, [wkey] + rkeys, [pk])
            evac(pb, pk)

        def dense_tm(lhs_fn, lkeys, kc_n, wview, wkey, col0, ncols, bs, evac):
            pb, pk = nbank()
            for kc in range(kc_n):
                MM(pb[:bs, 0:ncols], lhs_fn(kc), wview[:, kc, col0:col0 + ncols], kc == 0, kc == kc_n - 1, [wkey] + lkeys, [pk])
            evac(pb, pk)

        def ring_runs(blk0, nb):
            runs = []
            t = 0
            while t < nb:
                s0 = (blk0 + t) % 8
                c = min(nb - t, 8 - s0)
                runs.append((t, c, s0))
                t += c
            return runs

        def attention(l, t, nq, qcol, kblocks_b, kblocks_a, sample):
            for kind in ('a', 'b'):
                kbl = kblocks_a if kind == 'a' else kblocks_b
                nkb = len(kbl)
                for j in range(4):
                    pi = rot['pt'] % 2
                    rot['pt'] += 1
                    P_ = PT[pi]
                    pkey = ('PT', pi)
                    qa = j if kind == 'a' else 4 + j
                    for e in range(2):
                        h = 2 * j + e
                        sb_, sk = (psS[0], PSK[0]) if e == 0 else (psS[2], PSK[2])
                        for jj, (slot, nk, _, _) in enumerate(kbl):
                            if jj < 4:
                                out = sb_[:nk, jj * 128:jj * 128 + nq]
                                ok = sk
                            else:
                                out = psS[1][:nk, e * 128:e * 128 + nq]
                                ok = PSK[1]
                            if kind == 'a':
                                lk = kTa[64 * e:64 * e + 64, j // 2, slot * 128:slot * 128 + nk]
                                kk = ('kTa', slot)
                            else:
                                lk = kTb[64 * e:64 * e + 64, j, slot * 128:slot * 128 + nk]
                                kk = ('kTb', slot)
                            MM(out, lk, qT[64 * e:64 * e + 64, qa, qcol:qcol + nq], True, False, [kk] + BK(0, range(4)), [ok])
                            if kind == 'a':
                                rb = cAs[:nk, jj, h, :] if sample else cA[:nk, jj, h, :]
                                MM(out, cI[:nk, :nk], rb, False, True, ['cI', 'cA', 'cAs'], [ok])
                            else:
                                if sample:
                                    MM(out, hank[0:16, h, jj * 128:jj * 128 + nk], cJ16[:, :], False, True, ['hank', 'cJ16'], [ok])
                                else:
                                    msk = jj in (0, 4)
                                    MM(out, hank[:, h, jj * 128:jj * 128 + nk], cJ[:, :], False, not msk, ['hank', 'cJ'], [ok])
                                    if msk:
                                        MM(out, cI[:, :], cMB[:, 0 if jj == 0 else 1, :], False, True, ['cI', 'cMB'], [ok])
                        nmain = min(nkb, 4)
                        if sample:
                            for jj, (slot, nk, _, _) in enumerate(kbl):
                                if jj < 4:
                                    ACT(P_[:nk, jj, e * 128:e * 128 + nq], sb_[:nk, jj * 128:jj * 128 + nq], AF.Exp, [sk], [pkey])
                                else:
                                    ACT(P_[:nk, jj, e * 128:e * 128 + nq], psS[1][:nk, e * 128:e * 128 + nq], AF.Exp, [PSK[1]], [pkey])
                        else:
                            ACT(P_[:, 0:nmain, e * 128:(e + 1) * 128], sb_[:, 0:nmain * 128].rearrange("p (a b) -> p a b", b=128),
                                AF.Exp, [sk], [pkey])
                            if nkb > 4:
                                ACT(P_[:, 4, e * 128:(e + 1) * 128], psS[1][:, e * 128:(e + 1) * 128], AF.Exp, [PSK[1]], [pkey])
                    pb, pk = nbank()
                    W2 = 128 + nq
                    for jj, (slot, nk, _, _) in enumerate(kbl):
                        if kind == 'a':
                            lv = va[:nk, slot, (j // 2) * 128:(j // 2) * 128 + 128]
                            vk = ('va', slot)
                        else:
                            lv = vb[:nk, slot, j * 128:(j + 1) * 128]
                            vk = ('vb', slot)
                        MM(pb[:, 0:W2], lv, P_[:nk, jj, 0:W2], jj == 0, jj == nkb - 1, [vk, pkey], [pk])
                    for jj, (slot, nk, vo, vokey) in enumerate(kbl):
                        MM(pb[:, 256:256 + W2], vo[:nk, :], P_[:nk, jj, 0:W2], jj == 0, jj == nkb - 1, [vokey, pkey], [pk])
                    if kind == 'a':
                        TS('dve', rden[:, 0:W2], pb[:, 256:256 + W2], esink[:, l * 4 + j:l * 4 + j + 1], None, ALU.add, None,
                           [pk, 'esink'], ['rden'])
                    else:
                        TS('dve', rden[:, 0:W2], pb[:, 256:256 + W2], 1e-30, None, ALU.add, None, [pk], ['rden'])
                    S.op('dve', (lambda W2=W2: (lambda e_: e_.reciprocal(out=rden[:, 0:W2], in_=rden[:, 0:W2])))(), ['rden'], ['rden'])
                    oi = j if kind == 'a' else 4 + j
                    TT('dve', oT[0:64, oi, qcol:qcol + nq], pb[0:64, 0:nq], rden[0:64, 0:nq], ALU.mult, [pk, 'rden'], BK(1, range(4)))
                    TT('dve', oT[64:128, oi, qcol:qcol + nq], pb[64:128, 128:128 + nq], rden[64:128, 128:128 + nq], ALU.mult,
                       [pk, 'rden'], BK(1, range(4)))


        SETK = [[('pb', 3), ('pb', 4), ('pb', 5)], [('pb', 6), ('pb', 7), ('pb', 5)]]
        MAINB = [3 * 512, 6 * 512]
        J4B = [5 * 512, 5 * 512 + 256]

        def att_units(l, blk0, nb):
            units = []
            for t in range(nb):
                gb = blk0 + t
                for kind in ('a', 'b'):
                    if kind == 'a':
                        kbl = [((gb - 1 + i) % 8, gb - 1 + i) for i in range(2)]
                    else:
                        kbl = [((gb - 4 + i) % 8, gb - 4 + i) for i in range(5)]
                    for j in range(4):
                        units.append({'l': l, 'qcol': t * 128, 'kind': kind, 'j': j, 'kbl': kbl})
            for i, u in enumerate(units):
                u['set'] = i % 2
                u['pi'] = i % 2
            return units

        def u_col(u, e, jj):
            nkb = len(u['kbl'])
            nj = min(nkb, 4)
            if jj < nj:
                jp = (3 - jj) if u['kind'] == 'b' else jj
                return MAINB[u['set']] + e * nj * 128 + jp * 128
            return J4B[u['set']] + e * 128

        def att_S(u):
            kind, j, qcol = u['kind'], u['j'], u['qcol']
            qa = j if kind == 'a' else 4 + j
            base = MAINB[u['set']]
            nkb = len(u['kbl'])

            def emit(group):
                for k_, (out, lt, rh, rk, ok) in enumerate(group):
                    MM(out, lt, rh, k_ == 0, k_ == len(group) - 1, rk, [ok])

            def qk_ops(e, jj):
                slot, blk = u['kbl'][jj]
                h = 2 * j + e
                c0 = u_col(u, e, jj)
                out = psum[:, c0:c0 + 128]
                ok = ('pb', c0 // 512)
                ops_ = []
                if kind == 'a':
                    ops_.append((out, kTa[64 * e:64 * e + 64, j // 2, slot * 128:slot * 128 + 128],
                                 qT[64 * e:64 * e + 64, qa, qcol:qcol + 128], [('kTa', slot), ('big', 0, qcol // 128)], ok))
                else:
                    ops_.append((out, kTb[64 * e:64 * e + 64, j, slot * 128:slot * 128 + 128],
                                 qT[64 * e:64 * e + 64, qa, qcol:qcol + 128], [('kTb', slot), ('big', 0, qcol // 128)], ok))
                if blk < NVON:
                    ops_.append((out, negrow[0:1, blk * 128:(blk + 1) * 128], ones[0:1, :], ['negrow', 'ones'], ok))
                return ops_
            for e in range(2):
                h = 2 * j + e
                if kind == 'a':
                    c0 = base + e * 256
                    grp = [(psum[:, c0:c0 + 256], cI[:, :], cA[:, h, :, :].rearrange("p a b -> p (a b)"), ['cI', 'cA'], ('pb', c0 // 512))]
                    for jj in range(nkb):
                        grp += qk_ops(e, jj)
                    emit(grp)
                else:
                    c0 = base + e * 512
                    grp = [(psum[:, c0:c0 + 512], cJ[:, :], hank[:, h, 128:640], ['hank', 'cJ'], ('pb', c0 // 512))]
                    for jj in range(4):
                        grp += qk_ops(e, jj)
                    emit(grp)
                    c1 = J4B[u['set']] + e * 128
                    grp = [(psum[:, c1:c1 + 128], cJ[:, :], hank[:, h, 0:128], ['hank', 'cJ'], ('pb', c1 // 512))]
                    grp += qk_ops(e, 4)
                    emit(grp)

        def att_EXP(u):
            nkb = len(u['kbl'])
            nj = min(nkb, 4)
            mb = MAINB[u['set']]
            Wm = 2 * nj * 128
            ACT(PTf[u['pi']][:, 0:Wm], psum[:, mb:mb + Wm], AF.Exp, [('pb', mb // 512), ('pb', (mb + Wm - 1) // 512)], [('PT', u['pi'])])
            if nkb > 4:
                jb = J4B[u['set']]
                ACT(PTf[u['pi']][:, Wm:Wm + 256], psum[:, jb:jb + 256], AF.Exp, [('pb', 5)], [('PT', u['pi'])])

        def att_PV(u):
            kind, j = u['kind'], u['j']
            nkb = len(u['kbl'])
            nj = min(nkb, 4)
            P_ = PTf[u['pi']]
            pkey = ('PT', u['pi'])
            pb, pk = nbank()

            def rhs_out(jj, c_out):
                if jj < nj:
                    jp = (3 - jj) if kind == 'b' else jj
                    r = P_[:, 0:2 * nj * 128].rearrange("p (e j q) -> p e j q", e=2, j=nj)[:, :, jp, :]
                    o = pb[:, c_out:c_out + 256].rearrange("p (e q) -> p e q", e=2)
                else:
                    r = P_[:, 2 * nj * 128:2 * nj * 128 + 256]
                    o = pb[:, c_out:c_out + 256]
                return r, o
            for jj, (slot, blk) in enumerate(u['kbl']):
                if kind == 'a':
                    lv = va[:, slot, (j // 2) * 128:(j // 2) * 128 + 128]
                    vk = ('va', slot)
                else:
                    lv = vb[:, slot, j * 128:(j + 1) * 128]
                    vk = ('vb', slot)
                r, o = rhs_out(jj, 0)
                MM(o, lv, r, jj == 0, jj == nkb - 1, [vk, pkey], [pk])
            for jj in range(nkb):
                r, o = rhs_out(jj, 256)
                MM(o, ones[:, :], r, jj == 0, jj == nkb - 1, ['ones', pkey], [pk])
            u['pb'] = (pb, pk)

        def att_NORM(u):
            kind, j, qcol, l = u['kind'], u['j'], u['qcol'], u['l']
            pb, pk = u['pb']
            for (r0, c0) in ((0, 256), (64, 384)):
                if kind == 'a':
                    TS('dve', rden[r0:r0 + 64, 0:128], pb[r0:r0 + 64, c0:c0 + 128], esink[r0:r0 + 64, l * 4 + j:l * 4 + j + 1], None,
                       ALU.add, None, [pk, 'esink'], ['rden'])
                else:
                    TS('dve', rden[r0:r0 + 64, 0:128], pb[r0:r0 + 64, c0:c0 + 128], 1e-30, None, ALU.add, None, [pk], ['rden'])
            S.op('dve', lambda e_: e_.reciprocal(out=rden[:, 0:128], in_=rden[:, 0:128]), ['rden'], ['rden'])
            oi = j if kind == 'a' else 4 + j
            TT('dve', oT[0:64, oi, qcol:qcol + 128], pb[0:64, 0:128], rden[0:64, 0:128], ALU.mult, [pk, 'rden'], [('big', 1, qcol // 128)])
            TT('dve', oT[64:128, oi, qcol:qcol + 128], pb[64:128, 128:256], rden[64:128, 0:128], ALU.mult, [pk, 'rden'], [('big', 1, qcol // 128)])

        def attention_prompt(l, blk0, nb):
            units = att_units(l, blk0, nb)
            att_S(units[0])
            for i, u in enumerate(units):
                if i + 1 < len(units):
                    att_S(units[i + 1])
                att_EXP(u)
                att_PV(u)
                att_NORM(u)

        def load_g(l, which, gi):
            src = {'mix': g_mix, 'x': g_xattn, 'ffn': g_ffn, 'mem': g_mem}[which]
            DMA('sp', g_bc[:, gi, :], bass.AP(src, l * D, [[0, 128], [1, D]]), [], [('g', gi)])

        def kv_project(l, blk0, nb, bs, sample):
            n = nb * bs if not sample else bs
            hk = hT_keys(nb)
            wv_, wk_ = wload([(s_wk, l, 0, 512, 0)], 8, 512)
            for f in range(4):
                def ev(pb, pk, f=f):
                    if sample:
                        CP(cpeng(), kTb[:, f, 4 * 128:4 * 128 + bs], pb[:, 0:bs], [pk], [('kTb', 4)])
                    else:
                        for (t0, c, s0) in ring_runs(blk0, nb):
                            CP(cpeng(), kTb[:, f, s0 * 128:(s0 + c) * 128], pb[:, t0 * 128:(t0 + c) * 128], [pk],
                               [('kTb', s0 + i) for i in range(c)])
                dense_fm(wv_, wk_, 8, f * 128, lambda kc: hT[:, kc, 0:n], hk, n, ev)
            wv_, wk_ = wload([(s_wk, l, 512, 256, 0)], 8, 256)
            for g in range(2):
                def ev(pb, pk, g=g):
                    if sample:
                        CP(cpeng(), kTa[:, g, 4 * 128:4 * 128 + bs], pb[:, 0:bs], [pk], [('kTa', 4)])
                    else:
                        for (t0, c, s0) in ring_runs(blk0, nb):
                            CP(cpeng(), kTa[:, g, s0 * 128:(s0 + c) * 128], pb[:, t0 * 128:(t0 + c) * 128], [pk],
                               [('kTa', s0 + i) for i in range(c)])
                dense_fm(wv_, wk_, 8, g * 128, lambda kc: hT[:, kc, 0:n], hk, n, ev)
            wv_, wk_ = wload([(s_wv, l, 0, 512, 0)], 8, 512)
            wv2, wk2 = wload([(s_wv, l, 512, 256, 0)], 8, 256)
            for t in range(nb):
                slot = 4 if sample else (blk0 + t) % 8
                gb = blk0 + t

                def evb(pb, pk, slot=slot, gb=gb):
                    if sample:
                        CP(cpeng(), vb[:bs, slot, :], pb[:bs, 0:512], [pk], [('vb', slot)])
                    else:
                        TS('dve', vb[:, slot, :], pb[:, 0:512], vblk[:, gb:gb + 1], None, ALU.mult, None, [pk, 'vblk'], [('vb', slot)])

                def eva(pb, pk, slot=slot, gb=gb):
                    if sample:
                        CP(cpeng(), va[:bs, slot, :], pb[:bs, 0:256], [pk], [('va', slot)])
                    else:
                        TS('dve', va[:, slot, :], pb[:, 0:256], vblk[:, gb:gb + 1], None, ALU.mult, None, [pk, 'vblk'], [('va', slot)])
                dense_tm(lambda kc, t=t: hT[:, kc, t * 128:t * 128 + bs], [('hT', t)], 8, wv_, wk_, 0, 512, bs, evb)
                dense_tm(lambda kc, t=t: hT[:, kc, t * 128:t * 128 + bs], [('hT', t)], 8, wv2, wk2, 0, 256, bs, eva)

        def kv_outputs(l, blk0, nb, bs, sample):
            outb = [t for t in range(nb) if sample or blk0 + t >= 32]
            if not outb:
                return
            groups = [(0, 512), (512, 512), (1024, 256)]
            wl = [wload([(s_wkvo, l, c0, nc_, 0)], 8, nc_) for (c0, nc_) in groups]
            for t in outb:
                for gi_, (c0, nc_) in enumerate(groups):
                    def ev(pb, pk, c0=c0, nc_=nc_):
                        CP(cpeng(), stage[:bs, c0:c0 + nc_], pb[:bs, 0:nc_], [pk], ['stage'])
                    dense_tm(lambda kc, t=t: hT[:, kc, t * 128:t * 128 + bs], [('hT', t)], 8, wl[gi_][0], wl[gi_][1], 0, nc_, bs, ev)
                if sample:
                    DMA('pool', o_skv.ap()[l, :, :], stage[:bs, :], ['stage'], [('o_skv', l)])
                else:
                    DMA('pool', o_pkv.ap()[l, blk0 + t - 32, :, :], stage[:, :], ['stage'], [('o_pkv', l, blk0 + t)])

        def load_x(l, blk0, nb, bs, sample):
            XB['i'] ^= 1; CUR['xb'] = XB['i']
            for t in range(nb):
                if sample:
                    src = xs_in.ap()[:, :] if l == 0 else xssc.ap()[:, :]
                    DMA('sp', xts[CUR['xb']][:bs, t, :], src, ['xssc'], [('xt', CUR['xb'], t)])
                else:
                    gb = blk0 + t
                    src = (xin if l == 0 else xsc).ap()[gb * 128:(gb + 1) * 128, :]
                    DMA('sp', xts[CUR['xb']][:, t, :], src, [('xsc', gb)], [('xt', CUR['xb'], t)])

        def tile_kv(l, blk0, nb):
            load_x(l, blk0, nb, 128, False)
            norm_to_hT(l, 0, nb, 128)
            kv_project(l, blk0, nb, 128, False)

        def sample_prep(l):
            DMA('pool', cst[:, :, :], cbk.ap()[l].rearrange("(a p) c -> p a c", p=128), [], ['stage'])
            for a in range(4):
                pt_, pk = ntr()
                for f in range(4):
                    TR(pt_[:, f * 128:(f + 1) * 128], cst[:, a, f * 128:(f + 1) * 128], cI[:, :], ['stage', 'cI'], [pk])
                CP(cpeng(), kTb[:, :, a * 128:(a + 1) * 128], pt_[:, 0:512].rearrange("p (f c) -> p f c", c=128), [pk], [('kTb', a)])
            DMA('pool', vb[:, 0:4, :], cbv.ap()[l].rearrange("(a p) c -> p a c", p=128), [], [('vb', i) for i in range(4)])
            cview = cst[:, 0, 0:256].rearrange("p (g d c) -> p g d c", g=2, d=2)
            for d_ in range(2):
                DMA('pool', cview[:, :, d_, :], cak.ap()[l].rearrange("p (g c) -> p g c", g=2), [], ['stage'])
            pt_, pk = ntr()
            for g in range(2):
                TR(pt_[:, g * 128:(g + 1) * 128], cst[:, 0, g * 128:(g + 1) * 128], cI[:, :], ['stage', 'cI'], [pk])
            CP(cpeng(), kTa[:, :, 3 * 128:4 * 128], pt_[:, 0:256].rearrange("p (g c) -> p g c", c=128), [pk], [('kTa', 3)])
            vview = va[:, 3, :].rearrange("p (g d c) -> p g d c", g=2, d=2)
            for d_ in range(2):
                DMA('pool', vview[:, :, d_, :], cav.ap()[l].rearrange("p (g c) -> p g c", g=2), [], [('va', 3)])
            DMA('pool', cst[:, 0:2, :], cmk.ap()[l].rearrange("(a p) c -> p a c", p=128), [], ['stage'])
            for a in range(2):
                pt_, pk = ntr()
                for h in range(4):
                    TR(pt_[:, h * 128:(h + 1) * 128], cst[:, a, h * 128:(h + 1) * 128], cI[:, :], ['stage', 'cI'], [pk])
                CP(cpeng(), mkT[:, :, a * 128:(a + 1) * 128], pt_[:, 0:512].rearrange("p (f c) -> p f c", c=128), [pk], ['mkT'])
            DMA('pool', mv[:, :, :], cmv.ap()[l].rearrange("(a p) c -> p a c", p=128), [], ['mv'])
            DMA('sp', uh[:, :, :], sconv_in.ap()[l], [], ['uh'])
            DMA('pool', hank[0:16, :, 0:528], bass.AP(ext, l * 8 * 768 + 112, [[1, 16], [768, 8], [1, 528]]), PK[(l, 'ext')] + [('extc', l)], ['hank'], slow=True)

        def mem_kv(l):
            sub = dbg[2] if (dbg is not None and dbg[1] == 1) else 99
            load_g(l, 'mem', 0)
            if sub < 1:
                return
            XB['i'] ^= 1; CUR['xb'] = XB['i']
            for t in range(2):
                DMA('sp', xts[CUR['xb']][:, t, :], mem_in.ap()[t * 128:(t + 1) * 128, :], [], [('xt', CUR['xb'], t)])
            if sub < 2:
                return
            norm_to_hT(l, 0, 2, 128)
            if sub < 3:
                return
            hk = hT_keys(2)
            w1, k1 = wload([(s_wxkv, l, 0, 512, 0)], 8, 512)
            w2, k2 = wload([(s_wxkv, l, 512, 512, 0)], 8, 512)
            if sub < 4:
                return
            for h in range(4):
                def ev(pb, pk, h=h):
                    CP(cpeng(), mkT[:, h, :], pb[:, 0:256], [pk], ['mkT'])
                dense_fm(w1, k1, 8, h * 128, lambda kc: hT[:, kc, 0:256], hk, 256, ev)
            if sub < 5:
                return
            for t in range(2):
                def evk(pb, pk):
                    CP(cpeng(), stage[:, 0:512], pb[:, 0:512], [pk], ['stage'])

                def evv(pb, pk, t=t):
                    import os
                    v_ = os.environ.get('EVV', 'ab')
                    if 'a' in v_:
                        CP('act', stage[:, 512:1024], pb[:, 0:512], [pk], ['stage'])
                    if 'b' in v_:
                        CP('dve', mv[:, t, :], pb[:, 0:512], [pk], ['mv'])
                    if 'c' in v_:
                        CP('dve', stage[:, 512:1024], pb[:, 0:512], [pk], ['stage'])
                dense_tm(lambda kc, t=t: hT[:, kc, t * 128:(t + 1) * 128], [('hT', t)], 8, w1, k1, 0, 512, 128, evk)
                if sub < 6:
                    return
                dense_tm(lambda kc, t=t: hT[:, kc, t * 128:(t + 1) * 128], [('hT', t)], 8, w2, k2, 0, 512, 128, evv)
                if sub < 7:
                    return
                DMA('pool', o_pm.ap()[l, t * 128:(t + 1) * 128, :], stage[:, 0:1024], ['stage'], [('o_pm', l, t)])
                if sub < 8:
                    return

        def tile_head(l, blk0, nb, sample=False):
            bs = 16 if sample else 128
            load_x(l, blk0, nb, bs, sample)
            norm_to_hT(l, 0, nb, bs)
            return CUR['xb']

        def tile_full(l, blk0, nb, xb, sample=False, last=False):
            bs = 16 if sample else 128
            n = bs if sample else nb * 128
            hk = hT_keys(nb)
            CUR['xb'] = xb
            for half in range(2):
                wv_, wk_ = wload([(s_wq, l, half * 512, 512, 0)], 8, 512)
                for f in range(4):
                    def ev(pb, pk, f=f, half=half):
                        SCL('dve' if f % 2 else 'act', qT[:, half * 4 + f, 0:n], pb[:, 0:n], 0.125, [pk], BK(0, range(4)))
                    dense_fm(wv_, wk_, 8, f * 128, lambda kc: hT[:, kc, 0:n], hk, n, ev)
            kv_project(l, blk0, nb, bs, sample)
            kv_outputs(l, blk0, nb, bs, sample)
            if sample:
                kb_b = [(a, 128, ones, 'ones') for a in range(4)] + [(4, 16, ones, 'ones')]
                kb_a = [(3, 128, ones, 'ones'), (4, 16, ones, 'ones')]
                attention(l, 0, 16, 0, kb_b, kb_a, True)
            else:
                attention_prompt(l, blk0, nb)
            for half in range(2):
                wga, kga = wload([(s_wg, l, half * 512, 512, 0)], 8, 512)
                wgb, kgb = wload([(s_wg, l, 1024 + half * 512, 512, 0)], 8, 512)
                wo, ko = wload([(s_woa, l, half * 512, 512, 0), (s_wob, l, half * 512, 512, 512)], 4, 1024)
                for f in range(4):
                    fo = half * 4 + f

                    def ev_sa(pb, pk):
                        ACT(accg[:, 0:n], pb[:, 0:n], AF.Sigmoid, [pk], [('accg', 0)])

                    def ev_pa(pb, pk):
                        TT('dve', ug[:, 0:n], pb[:, 0:n], accg[:, 0:n], ALU.mult, [pk, ('accg', 0)], [('ug', 0)])

                    def ev_sb(pb, pk):
                        ACT(accu[:, 0:n], pb[:, 0:n], AF.Sigmoid, [pk], [('accu', 0)])

                    def ev_pb(pb, pk, fo=fo):
                        TT('dve', uu[:, 0:n], pb[:, 0:n], accu[:, 0:n], ALU.mult, [pk, ('accu', 0)], [('uu', 0)])
                        TT('dve', mT[:, fo, 0:n], ug[:, 0:n], uu[:, 0:n], ALU.add, [('ug', 0), ('uu', 0)], BK(2, range(4)))
                    dense_fm(wga, kga, 8, f * 128, lambda kc: hT[:, kc, 0:n], hk, n, ev_sa)
                    dense_fm(wo, ko, 4, f * 128, lambda kc: oT[:, kc, 0:n], BK(1, range(4)), n, ev_pa)
                    dense_fm(wgb, kgb, 8, f * 128, lambda kc: hT[:, kc, 0:n], hk, n, ev_sb)
                    dense_fm(wo, ko, 4, 512 + f * 128, lambda kc: oT[:, 4 + kc, 0:n], BK(1, range(4)), n, ev_pb)
            wout_ = [wload([(s_wout, l, half * 512, 512, 0)], 8, 512) for half in range(2)]
            for t in range(nb):
                for half in range(2):
                    def ev(pb, pk, t=t, half=half):
                        TT('dve', xts[CUR['xb']][:bs, t, half * 512:(half + 1) * 512], pb[:bs, 0:512], xts[CUR['xb']][:bs, t, half * 512:(half + 1) * 512], ALU.add,
                           [pk, ('xt', CUR['xb'], t)], [('xt', CUR['xb'], t)])
                    dense_tm(lambda kc, t=t: mT[:, kc, t * 128:t * 128 + bs], [('big', 2, t)], 8, wout_[half][0], wout_[half][1], 0, 512, bs, ev)
                norm_stats(1, bs, t)
                norm_scale(1, bs, t)
            norm_tr(bs, range(nb))
            halves = [(0, 1)] if (sample or nb == 1) else [(0, nb)]

            def hcols(hf):
                return hf[0] * 128, (hf[0] * 128 + bs) if (sample or nb == 1) else hf[1] * 128
            wv_, wk_ = wload([(s_wxq, l, 0, 512, 0)], 8, 512)
            for h in range(4):
                for hf in halves:
                    c0, c1 = hcols(hf)

                    def ev(pb, pk, h=h, c0=c0, c1=c1, hf=hf):
                        TS('dve', qT[:, h, c0:c1], pb[:, 0:c1 - c0], 128.0 ** -0.5, None, ALU.mult, None, [pk], BK(0, range(hf[0], hf[1])))
                    dense_fm(wv_, wk_, 8, h * 128, lambda kc, c0=c0, c1=c1: hT[:, kc, c0:c1], [('hT', t) for t in range(hf[0], hf[1])], c1 - c0, ev)
            SBK = [[(bank(5), ('pb', 5)), (bank(6), ('pb', 6))], [(bank(7), ('pb', 7)), (bank(4), ('pb', 4))]]
            PTxh = [PTx0[:, :, :]] if len(halves) == 1 else [PTx0[:, :, 0:256], PTx0[:, :, 256:512]]
            for h in range(4):
                for hi, hf in enumerate(halves):
                    c0, c1 = hcols(hf)
                    w = c1 - c0
                    for a in range(2):
                        sbk, skk = SBK[hi][a]
                        MM(sbk[:, 0:w], mkT[:, h, a * 128:(a + 1) * 128], qT[:, h, c0:c1], True, True, ['mkT'] + BK(0, range(hf[0], hf[1])), [skk])
                        ACT(PTxh[hi][:, a, 0:w], sbk[:, 0:w], AF.Exp, [skk], [('PTx', hi)])
                for hi, hf in enumerate(halves):
                    c0, c1 = hcols(hf)
                    w = c1 - c0
                    Px = PTxh[hi]
                    pxk = ('PTx', hi)
                    pO, pOk = nbank()
                    pD, pDk = nbank()
                    for a in range(2):
                        MM(pO[:, 0:w], mv[:, a, h * 128:(h + 1) * 128], Px[:, a, 0:w], a == 0, a == 1, ['mv', pxk], [pOk])
                    for a in range(2):
                        MM(pD[:, 0:w], ones[:, :], Px[:, a, 0:w], a == 0, a == 1, ['ones', pxk], [pDk])
                    S.op('dve', (lambda pD=pD, c0=c0, c1=c1, w=w: (lambda e_: e_.reciprocal(out=rden[:, c0:c1], in_=pD[:, 0:w])))(), [pDk], ['rden'])
                    TT('dve', oT[:, h, c0:c1], pO[:, 0:w], rden[:, c0:c1], ALU.mult, [pOk, 'rden'], BK(1, range(hf[0], hf[1])))
            wv_, wk_ = wload([(s_wxo, l, 0, 1024, 0)], 4, 1024)
            for t in range(nb):
                for half in range(2):
                    def ev(pb, pk, t=t, half=half):
                        TT('dve', xts[CUR['xb']][:bs, t, half * 512:(half + 1) * 512], pb[:bs, 0:512], xts[CUR['xb']][:bs, t, half * 512:(half + 1) * 512], ALU.add,
                           [pk, ('xt', CUR['xb'], t)], [('xt', CUR['xb'], t)])
                    dense_tm(lambda kc, t=t: oT[:, kc, t * 128:t * 128 + bs], [('big', 1, t)], 4, wv_, wk_, half * 512, 512, bs, ev)
                norm_stats(2, bs, t)
                norm_scale(2, bs, t)
            norm_tr(bs, range(nb))
            lastgb = blk0 + nb - 1
            for i in range(11):
                wv_, wk_ = wload([(s_wup, l, 256 * i, 256, 0), (s_wup, l, DFF + 256 * i, 256, 256)], 8, 512)
                for e in range(2):
                    fbk = 2 * i + e
                    bi = fbk % 2
                    for (ub, ac, uk, ak, col0, fidx) in ((ugs[bi], accgs[bi], ('ug', bi), ('accg', bi), e * 128, fbk),
                                                         (uus[bi], accus[bi], ('uu', bi), ('accu', bi), 256 + e * 128, 22 + fbk)):
                        wbase = (l * 3) * 44 + fidx

                        def ev(pb, pk, ub=ub, uk=uk, ac=ac, ak=ak, wbase=wbase, fidx=fidx):
                            CP('act', ub[:, 2:2 + n], pb[:, 0:n], [pk], [uk])
                            ACT(ac[:, 0:n], pb[:, 0:n], AF.Identity, [pk, 'wcv', 'bcv'], [ak],
                                bias=bcv[:, l * 44 + fidx:l * 44 + fidx + 1], scale=wcv[:, wbase + 88:wbase + 89])
                        dense_fm(wv_, wk_, 8, col0, lambda kc: hT[:, kc, 0:n], hk, n, ev)
                        CP('pool', ub[:, 0:2], uh[:, fidx, :], ['uh'], [uk])
                        STT('dve', ac[:, 0:n], ub[:, 1:1 + n], wcv[:, wbase + 44:wbase + 45], ac[:, 0:n], ALU.mult, ALU.add, [uk, ak, 'wcv'], [ak])
                        STT('dve', ac[:, 0:n], ub[:, 0:n], wcv[:, wbase:wbase + 1], ac[:, 0:n], ALU.mult, ALU.add, [uk, ak, 'wcv'], [ak])
                        if last or sample:
                            CP('pool', ulast[:, fidx, :], ub[:, n:n + 2], [uk], ['ulast'])
                        if not sample:
                            TS('pool', uh[:, fidx, :], ub[:, n:n + 2], vblk[:, lastgb:lastgb + 1], None, ALU.mult, None, [uk, 'vblk'], ['uh'])
                    ACT(accgs[bi][:, 0:n], accgs[bi][:, 0:n], AF.Silu, [('accg', bi)], [('accg', bi)])
                    TT('dve', m2T[:, fbk, 0:n], accgs[bi][:, 0:n], accus[bi][:, 0:n], ALU.mult, [('accg', bi), ('accu', bi)], m2key(fbk))
            if last or sample:
                DMA('pool', (o_sconv if sample else o_pconv).ap()[l], ulast[:, :, :], ['ulast'], [('o_conv', sample, l)])
            for ch in range(2):
                groups = [(0, 8), (8, 8), (16, 6)]
                for gi_, (k0, nk_) in enumerate(groups):
                    wv_, wk_ = wload([(s_wdn, l, ch * 512, 512, 0)], nk_, 512, krow0=k0)
                    for t in range(nb):
                        for kc in range(nk_):
                            MM(bank(4 + t)[:bs, 0:512], m2T[:, k0 + kc, t * 128:t * 128 + bs], wv_[:, kc, 0:512],
                               gi_ == 0 and kc == 0, gi_ == 2 and kc == nk_ - 1, [wk_] + BIG, [('pb', 4 + t)])
                if ch == 1:
                    yield
                    CUR['xb'] = xb
                for t in range(nb):
                    TT('dve', xts[CUR['xb']][:bs, t, ch * 512:(ch + 1) * 512], bank(4 + t)[:bs, 0:512], xts[CUR['xb']][:bs, t, ch * 512:(ch + 1) * 512],
                       ALU.add, [('pb', 4 + t), ('xt', CUR['xb'], t)], [('xt', CUR['xb'], t)])
            for t in range(nb):
                if sample:
                    DMA('pool', xssc.ap()[:, :], xts[CUR['xb']][:bs, t, :], [('xt', CUR['xb'], t)], ['xssc'])
                else:
                    gb = blk0 + t
                    DMA('pool', xsc.ap()[gb * 128:(gb + 1) * 128, :], xts[CUR['xb']][:, t, :], [('xt', CUR['xb'], t)], [('xsc', gb)])

        for l in range(DEPTH):
            DMA('sp', exb[:, 384:385], rel.ap()[l, :, 0:1], [], ['stage'], slow=True)
            for c in range(3):
                TS('dve', exb[:, c * 128:(c + 1) * 128], ones32[0:8, :], exb[:, 384:385], None, ALU.mult, None, ['stage', 'ones32'], ['stage'])
            DMA('pool', ext.ap()[l, :, 0:383], exb[:, 0:383], ['stage'], [('extc', l)])
            DMA('pool', rext.ap()[l, :, 384:767], exb[:, 0:383], ['stage'], [('extc', l)])
            DMA('pool', rext.ap()[l, :, 0:384], rel_rev.ap()[l, :, 129:513], [], [('extc', l)])
        do_prep(0)
        for l in range(DEPTH if dbg is None else dbg[0]):
            pending = prep_list(l + 1) if l + 1 < DEPTH else []
            ntl = (NBLK - FB[l] + 3) // 4 + 1
            per = (len(pending) + max(1, ntl - 3) - 1) // max(1, ntl - 3) if pending else 0

            def more_prep():
                for _ in range(per):
                    if pending:
                        dst, src, key = pending.pop(0)
                        DMA('pool', dst, src, [], [key], slow=True)
            if dbg is not None and dbg[1] < 1:
                break
            mem_kv(l)
            if dbg is not None and dbg[1] < 2:
                break
            load_g(l, 'mix', 0)
            load_g(l, 'x', 1)
            load_g(l, 'ffn', 2)
            for h in range(8):
                DMA('pool', hank[:, h, :], bass.AP(rext, (l * 8 + h) * 768, [[1, 128], [1, 640]]), PK[(l, 'ext')] + [('extc', l)], ['hank'])
            for h in range(8):
                TT('dve', hank[:, h, 512:640], hank[:, h, 512:640], cMB[:, 0, :], ALU.add, ['hank', 'cMB'], ['hank'])
                TT('dve', hank[:, h, 0:128], hank[:, h, 0:128], cMB[:, 1, :], ALU.add, ['hank', 'cMB'], ['hank'])
            S.op('pool', lambda e: e.memset(uh[:, :, :], 0.0), [], ['uh'])
            fb = FB[l]
            tile_kv(l, fb - 4, 4)
            if dbg is not None and dbg[1] < 3:
                break
            tiles = []
            b0 = fb
            while b0 < NBLK:
                b1 = min(NBLK, (b0 // 4 + 1) * 4)
                tiles.append((b0, b1 - b0, b1 == NBLK))
                b0 = b1
            prev = None
            for (tb0, tnb, tlast) in tiles:
                more_prep()
                xb_ = tile_head(l, tb0, tnb)
                if prev is not None:
                    for _ in prev:
                        pass
                g_ = tile_full(l, tb0, tnb, xb_, last=tlast)
                next(g_)
                prev = g_
            for _ in prev:
                pass
            while pending:
                more_prep()
            sample_prep(l)
            xb_ = tile_head(l, 0, 1, sample=True)
            for _ in tile_full(l, 0, 1, xb_, sample=True):
                pass
        if dbg is None or dbg[1] >= 7:
          DMA('sp', g_bc[:, 0, :], bass.AP(g_final, 0, [[0, 128], [1, D]]), [], [('g', 0)])

        def final_norm(nb, bs, src_fn, dst_fn, key_fn):
            XB['i'] ^= 1; CUR['xb'] = XB['i']
            for t in range(nb):
                DMA('sp', xts[CUR['xb']][:bs, t, :], src_fn(t), [key_fn(t)], [('xt', CUR['xb'], t)])
                ACT(hn[:bs, t, :], xts[CUR['xb']][:bs, t, :], AF.Square, [('xt', CUR['xb'], t)], [('hn', t), ('ss', t)], accum=ss[:bs, t:t + 1])
                ACT(rstd[:bs, t:t + 1], ss[:bs, t:t + 1], AF.Sqrt, [('ss', t), 'epsT'], [('rstd', t)], bias=epsT[:bs, :], scale=1.0 / D)
                S.op('dve', (lambda t=t: (lambda e: e.reciprocal(out=rstd[:bs, t:t + 1], in_=rstd[:bs, t:t + 1])))(), [('rstd', t)], [('rstd', t)])
                STT('dve', stage[:bs, 0:D], xts[CUR['xb']][:bs, t, :], rstd[:bs, t:t + 1], g_bc[:bs, 0, :], ALU.mult, ALU.mult,
                    [('xt', CUR['xb'], t), ('rstd', t), ('g', 0)], ['stage'])
                DMA('pool', dst_fn(t), stage[:bs, 0:D], ['stage'], [('o_y', id(dst_fn), t)])
        for b0 in (range(20, NBLK, 4) if (dbg is None or dbg[1] >= 7) else []):
            final_norm(4, 128,
                       lambda t, b0=b0: xsc.ap()[(b0 + t) * 128:(b0 + t + 1) * 128, :],
                       lambda t, b0=b0: o_y.ap()[(b0 - 20 + t) * 128:(b0 - 20 + t + 1) * 128, :],
                       lambda t, b0=b0: ('xsc', b0 + t))
        if dbg is None or dbg[1] >= 7:
            final_norm(1, 16, lambda t: xssc.ap()[:, :], lambda t: o_ys.ap()[:, :], lambda t: 'xssc')

        rings = {'sp': 24, 'pool': 24}
        S_rings = {q: list(range(n_)) for q, n_ in rings.items()}
        S.finalize(S_rings)
        semh = {}
        for e_ in COMPUTE:
            semh[('eng', e_)] = es.enter_context(nc.semaphore("s_" + e_))
        for q, n_ in rings.items():
            for i in range(n_):
                semh[('ring', q, i)] = es.enter_context(nc.semaphore("r_%s%d" % (q, i)))
        with nc.Block() as block:
            @block.tensor
            def _(e):
                S.emit_engine('pe', e, semh)

            @block.scalar
            def _(e):
                S.emit_engine('act', e, semh)

            @block.vector
            def _(e):
                S.emit_engine('dve', e, semh)

            @block.gpsimd
            def _(e):
                S.emit_engine('pool', e, semh)

            @block.sync
            def _(e):
                S.emit_engine('sp', e, semh)
    return nc, len(S.ops)


def _consts(core):
    I = np.eye(128, dtype=np.float32)
    J = np.ascontiguousarray(I[::-1])
    J16 = np.ascontiguousarray(np.eye(16, dtype=np.float32)[::-1])
    slopes = np.exp2(-8.0 * np.arange(1, 9) / 8).astype(np.float32)
    s = np.arange(128)[:, None]
    q = np.arange(128)[None, :]
    cA = np.zeros((128, 8, 2, 128), np.float32)
    for j in range(2):
        dist = np.abs(128 * (1 - j) + q - s).astype(np.float32)
        ck = 2 * (j - 1) + s // 64
        cq = q // 64
        valid = (ck <= cq) & (ck >= cq - 2)
        for h in range(8):
            cA[:, h, j, :] = np.where(valid, -slopes[h] * dist, NEGM)
    cAs = np.zeros((128, 2, 8, 16), np.float32)
    q16 = np.arange(16)[None, :]
    for j in range(2):
        posk = (896 + s) if j == 0 else (1024 + s)
        dist = np.abs(1024 + q16 - posk).astype(np.float32)
        for h in range(8):
            cAs[:, j, h, :] = -slopes[h] * dist
    cMB = np.zeros((128, 2, 128), np.float32)
    cMB[:, 0, :] = np.where((s < 64) & (q >= 64), NEGM, 0.0)
    cMB[:, 1, :] = np.where((s >= 64) & (q < 64), NEGM, 0.0)
    vb = ((2048 * core - HALO + 128 * np.arange(NBLK)) >= 0).astype(np.float32)
    vblk = np.ascontiguousarray(np.broadcast_to(vb[None, :], (128, NBLK)))
    negrow = np.repeat(((1.0 - vb[:NVON]) * NEGM).astype(np.float32), 128)[None, :]
    return dict(c_negrow=np.ascontiguousarray(negrow), c_I=I, c_J=J, c_J16=J16, c_A=cA.reshape(128, -1), c_As=cAs.reshape(128, -1), c_MB=np.ascontiguousarray(cMB[::-1]).reshape(128, -1), c_vblk=vblk)


_NC = None
_DBG_HOOK = None


def kernel(x_prompt, x_sample, mem_prompt, cache_a_k, cache_a_v, cache_b_k, cache_b_v,
           cache_mem_k, cache_mem_v, state_conv, g_mix, w_mix_in, a_sink, b_rel_bias,
           w_o_a, w_o_b, w_mix_out, g_xattn, g_mem, w_xq, w_xk, w_xv, w_xo,
           g_ffn, w_up, w_conv, b_conv, w_down, g_final):
    global _NC
    f = lambda a: np.ascontiguousarray(np.asarray(a, dtype=np.float32))
    if _NC is None:
        _NC = build()[0]
    nc = _NC
    xp = f(x_prompt)[0]
    xpad = np.concatenate([np.zeros((HALO, D), np.float32), xp], axis=0)
    sink = f(a_sink)
    sink_l = np.zeros((128, 16), np.float32)
    for l in range(DEPTH):
        for j in range(4):
            sink_l[0:64, l * 4 + j] = sink[l, 2 * j]
            sink_l[64:128, l * 4 + j] = sink[l, 2 * j + 1]
    wc = f(w_conv).reshape(DEPTH, 3, 44, 128).transpose(3, 0, 1, 2).reshape(128, -1)
    bc = f(b_conv).reshape(DEPTH, 44, 128).transpose(2, 0, 1).reshape(128, -1)
    shared = dict(mem=f(mem_prompt)[0], g_mix=f(g_mix), g_xattn=f(g_xattn), g_mem=f(g_mem), g_ffn=f(g_ffn),
                  g_final=f(g_final).reshape(1, D), w_mix_in=f(w_mix_in), sink_l=sink_l, b_rel_bias=f(b_rel_bias), rel_rev=np.ascontiguousarray(f(b_rel_bias)[:, :, ::-1]),
                  w_o_a=f(w_o_a), w_o_b=f(w_o_b), w_mix_out=f(w_mix_out), w_xq=f(w_xq), w_xk=f(w_xk), w_xv=f(w_xv),
                  w_xo=f(w_xo), w_up=f(w_up), w_down=f(w_down), wconv_l=np.ascontiguousarray(wc), bconv_l=np.ascontiguousarray(bc))
    in_maps = []
    for c in range(8):
        m = dict(shared)
        m.update(_consts(c))
        m['xin'] = np.ascontiguousarray(xpad[2048 * c:2048 * c + LTOK])
        m['xs'] = f(x_sample)[c]
        m['cak'] = f(cache_a_k)[:, c].reshape(DEPTH, 128, 128)
        m['cav'] = f(cache_a_v)[:, c].reshape(DEPTH, 128, 128)
        m['cbk'] = f(cache_b_k)[:, c].reshape(DEPTH, 512, 512)
        m['cbv'] = f(cache_b_v)[:, c].reshape(DEPTH, 512, 512)
        m['cmk'] = f(cache_mem_k)[:, c].reshape(DEPTH, 256, 512)
        m['cmv'] = f(cache_mem_v)[:, c].reshape(DEPTH, 256, 512)
        m['sconv_in'] = np.ascontiguousarray(f(state_conv)[:, c].reshape(DEPTH, 2, 44, 128).transpose(0, 3, 2, 1))
        in_maps.append({k: np.ascontiguousarray(v) for k, v in m.items()})
    if _DBG_HOOK is not None:
        return _DBG_HOOK(in_maps)
    res = run_bass_kernel_spmd(nc, in_maps, core_ids=list(range(8))).results
    R = [{k: np.asarray(v, dtype=np.float32) for k, v in r.items()} for r in res]
    y_prompt = np.concatenate([R[c]['o_y'] for c in range(8)], axis=0)[None]
    y_sample = np.stack([R[c]['o_ys'] for c in range(8)], axis=0)
    pkv = R[7]['o_pkv']
    pa_k = pkv[:, 3, :, 0:128].reshape(DEPTH, 1, 128, 2, 64)
    pa_v = pkv[:, 3, :, 128:256].reshape(DEPTH, 1, 128, 2, 64)
    pb_k = pkv[:, :, :, 256:768].reshape(DEPTH, 1, 512, 8, 64)
    pb_v = pkv[:, :, :, 768:1280].reshape(DEPTH, 1, 512, 8, 64)
    pm = R[0]['o_pm']
    pm_k = pm[:, :, 0:512].reshape(DEPTH, 1, 256, 4, 128)
    pm_v = pm[:, :, 512:1024].reshape(DEPTH, 1, 256, 4, 128)
    cv = lambda a: a.transpose(0, 3, 2, 1).reshape(DEPTH, 2, 5632)
    pconv = cv(R[7]['o_pconv'])[:, None]
    skv = np.stack([R[c]['o_skv'] for c in range(8)], axis=1)
    sa_k = skv[..., 0:128].reshape(DEPTH, 8, 16, 2, 64)
    sa_v = skv[..., 128:256].reshape(DEPTH, 8, 16, 2, 64)
    sb_k = skv[..., 256:768].reshape(DEPTH, 8, 16, 8, 64)
    sb_v = skv[..., 768:1280].reshape(DEPTH, 8, 16, 8, 64)
    sconv = np.stack([cv(R[c]['o_sconv']) for c in range(8)], axis=1)
    c32 = lambda a: np.ascontiguousarray(a, dtype=np.float32)
    return tuple(c32(a) for a in (y_prompt, y_sample, pa_k, pa_v, pb_k, pb_v, pm_k, pm_v, pconv, sa_k, sa_v, sb_k, sb_v, sconv))
```

```python
import contextlib
import numpy as np
import concourse.bass as bass
import concourse.mybir as mybir
from concourse.bass_utils import run_bass_kernel_spmd

F32 = mybir.dt.float32
BF16 = mybir.dt.bfloat16
ALU = mybir.AluOpType
AF = mybir.ActivationFunctionType

DEPTH = 4
D = 1024
NBLK = 36
FB = [4, 9, 14, 19]
LTOK = NBLK * 128
HALO = 2560
DFF = 2816
NEGM = -30000.0
EPS = 1e-6
NVON = 20
COMPUTE = ('pe', 'act', 'dve', 'pool')


class Sched:
    def __init__(self):
        self.ops = []
        self.lastw = {}
        self.rd = {}

    def op(self, eng, fn, reads=(), writes=(), dma=False):
        i = len(self.ops)
        deps = set()
        writes = list(writes) + [k for k in reads if isinstance(k, tuple) and k[0] == 'pb']
        for k in reads:
            w = self.lastw.get(k)
            if w is not None:
                deps.add(w)
        for k in writes:
            w = self.lastw.get(k)
            if w is not None:
                deps.add(w)
            r = self.rd.get(k)
            if r:
                deps.update(r[0].values())
                deps.update(r[1])
        self.ops.append({'eng': eng, 'fn': fn, 'dma': dma, 'deps': deps, 'need': False})
        for k in reads:
            r = self.rd.setdefault(k, ({}, []))
            if dma:
                r[1].append(i)
            else:
                r[0][eng] = i
        for k in writes:
            self.lastw[k] = i
            self.rd[k] = ({}, [])
        return i

    def finalize(self, rings):
        ops = self.ops
        for o in ops:
            keep = set()
            for d in o['deps']:
                p = ops[d]
                if p['eng'] == 'pe' and o['eng'] == 'pe' and not p['dma'] and not o['dma']:
                    continue
                p['need'] = True
                keep.add(d)
            o['deps'] = keep
        cnt = {e: 0 for e in COMPUTE}
        dcount = {q: 0 for q in rings}
        for o in ops:
            if o['dma']:
                q = o['eng']
                k = dcount[q]
                dcount[q] += 1
                R = len(rings[q])
                o['sem'] = ('ring', q, k % R)
                o['val'] = 16 * (k // R + 1)
                o['gate'] = 16 * (k // R)
            elif o['need']:
                cnt[o['eng']] += 1
                o['sem'] = ('eng', o['eng'])
                o['val'] = cnt[o['eng']]
        self.final_ring = {}
        for o in ops:
            if o['dma']:
                self.final_ring[o['sem']] = max(self.final_ring.get(o['sem'], 0), o['val'])

    def emit_engine(self, engname, eng, semh):
        ops = self.ops
        waited = {}

        def wait(sem, val):
            if val > waited.get(sem, 0):
                eng.wait_ge(semh[sem], val)
                waited[sem] = val

        for o in ops:
            if o['eng'] != engname:
                continue
            need = {}
            for d in o['deps']:
                p = ops[d]
                need[p['sem']] = max(need.get(p['sem'], 0), p['val'])
            if o['dma'] and o['gate'] > 0:
                need[o['sem']] = max(need.get(o['sem'], 0), o['gate'])
            for s_, v_ in need.items():
                wait(s_, v_)
            ins = o['fn'](eng)
            if o['dma']:
                ins.then_inc(semh[o['sem']], 16)
            elif o['need']:
                ins.then_inc(semh[o['sem']], 1)
        for s_, v_ in self.final_ring.items():
            if s_[1] == engname:
                wait(s_, v_)


def build(dbg=None):
    nc = bass.Bass("TRN2", target_bir_lowering=False)
    S = Sched()

    def din(name, shape, dt=F32):
        return nc.dram_tensor(name, list(shape), dt, kind="ExternalInput")

    def dout(name, shape, dt=F32):
        return nc.dram_tensor(name, list(shape), dt, kind="ExternalOutput")

    def dscr(name, shape, dt=BF16):
        return nc.dram_tensor(name, list(shape), dt)

    xin = din("xin", [LTOK, D])
    xs_in = din("xs", [16, D])
    mem_in = din("mem", [256, D])
    cak = din("cak", [DEPTH, 128, 128]); cav = din("cav", [DEPTH, 128, 128])
    cbk = din("cbk", [DEPTH, 512, 512]); cbv = din("cbv", [DEPTH, 512, 512])
    cmk = din("cmk", [DEPTH, 256, 512]); cmv = din("cmv", [DEPTH, 256, 512])
    sconv_in = din("sconv_in", [DEPTH, 128, 44, 2])
    g_mix = din("g_mix", [DEPTH, D]); g_xattn = din("g_xattn", [DEPTH, D]); g_mem = din("g_mem", [DEPTH, D])
    g_ffn = din("g_ffn", [DEPTH, D]); g_final = din("g_final", [1, D])
    w_mix_in = din("w_mix_in", [DEPTH, D, 4352])
    sink_l = din("sink_l", [128, 16])
    rel = din("b_rel_bias", [DEPTH, 8, 513])
    rel_rev = din("rel_rev", [DEPTH, 8, 513])
    w_o_a = din("w_o_a", [DEPTH, 512, D]); w_o_b = din("w_o_b", [DEPTH, 512, D]); w_mix_out = din("w_mix_out", [DEPTH, D, D])
    w_xq = din("w_xq", [DEPTH, D, 512]); w_xk = din("w_xk", [DEPTH, D, 512]); w_xv = din("w_xv", [DEPTH, D, 512])
    w_xo = din("w_xo", [DEPTH, 512, D])
    w_up = din("w_up", [DEPTH, D, 2 * DFF]); w_down = din("w_down", [DEPTH, DFF, D])
    wconv_l = din("wconv_l", [128, DEPTH * 3 * 44]); bconv_l = din("bconv_l", [128, DEPTH * 44])
    c_I = din("c_I", [128, 128]); c_J = din("c_J", [128, 128]); c_J16 = din("c_J16", [16, 16])
    c_A = din("c_A", [128, 2 * 8 * 128]); c_As = din("c_As", [128, 2 * 8 * 16]); c_MB = din("c_MB", [128, 2 * 128])
    c_vblk = din("c_vblk", [128, NBLK])
    c_negrow = din("c_negrow", [1, (NVON + 3) * 128])

    o_y = dout("o_y", [2048, D]); o_ys = dout("o_ys", [16, D])
    o_pkv = dout("o_pkv", [DEPTH, 4, 128, 1280]); o_skv = dout("o_skv", [DEPTH, 16, 1280])
    o_pm = dout("o_pm", [DEPTH, 256, 1024])
    o_pconv = dout("o_pconv", [DEPTH, 128, 44, 2]); o_sconv = dout("o_sconv", [DEPTH, 128, 44, 2])

    xsc = dscr("xsc", [LTOK, D], F32); xssc = dscr("xssc", [16, D], F32)
    ext = dscr("ext", [DEPTH, 8, 768], F32)
    rext = dscr("rext", [DEPTH, 8, 768], F32)
    s_wq = dscr("s_wq", [DEPTH, D, 1024]); s_wk = dscr("s_wk", [DEPTH, D, 768]); s_wg = dscr("s_wg", [DEPTH, D, 2048])
    s_wv = dscr("s_wv", [DEPTH, D, 768]); s_wkvo = dscr("s_wkvo", [DEPTH, D, 1280])
    s_woa = dscr("s_woa", [DEPTH, 512, D]); s_wob = dscr("s_wob", [DEPTH, 512, D]); s_wout = dscr("s_wout", [DEPTH, D, D])
    s_wxq = dscr("s_wxq", [DEPTH, D, 512]); s_wxkv = dscr("s_wxkv", [DEPTH, D, 1024]); s_wxo = dscr("s_wxo", [DEPTH, 512, D])
    s_wup = dscr("s_wup", [DEPTH, D, 2 * DFF]); s_wdn = dscr("s_wdn", [DEPTH, DFF, D])

    es = contextlib.ExitStack()

    def sb(name, shape, dt):
        return es.enter_context(nc.sbuf_tensor(name, list(shape), dt))

    def pst_(name, shape, dt):
        return es.enter_context(nc.psum_tensor(name, list(shape), dt))

    with es:
        xts = [sb("xt%d" % i, [128, 4, D], F32) for i in range(2)]
        XB = {'i': 0}
        CUR = {'xb': 0}
        g_bc = sb("g_bc", [128, 3, D], F32)
        hn = sb("hn", [128, 4, D], BF16)
        hT = sb("hT", [128, 8, 512], BF16)
        big = sb("big", [128, 12288], BF16)
        kTb = sb("kTb", [128, 4, 1024], BF16); kTa = sb("kTa", [128, 2, 1024], BF16)
        vb = sb("vb", [128, 8, 512], BF16); va = sb("va", [128, 8, 256], BF16)
        PT = [sb("PT%d" % i, [128, 5, 256], BF16) for i in range(2)]
        PTf = [PT[i][:, :, :].rearrange("p a b -> p (a b)") for i in range(2)]
        PTx0 = sb("PTx0", [128, 2, 512], BF16)
        PTx = [PTx0, PTx0]
        mkT = sb("mkT", [128, 4, 256], BF16); mv = sb("mv", [128, 2, 512], BF16)
        hank = sb("hank", [128, 8, 640], BF16)
        cI = sb("cI", [128, 128], BF16); cJ = sb("cJ", [128, 128], BF16); cJ16 = sb("cJ16", [16, 16], BF16)
        ones = sb("ones", [128, 128], BF16)
        cA = sb("cA", [128, 8, 2, 128], BF16); cAs = sb("cAs", [128, 2, 8, 16], BF16); cMB = sb("cMB", [128, 2, 128], BF16)
        negrow = sb("negrow", [1, (NVON + 3) * 128], BF16)
        vblk = sb("vblk", [128, NBLK], F32)
        esink = sb("esink", [128, 16], F32)
        wcv = sb("wcv", [128, DEPTH * 3 * 44], F32); bcv = sb("bcv", [128, DEPTH * 44], F32)
        ugs = [sb("ug%d" % i, [128, 514], F32) for i in range(2)]; uus = [sb("uu%d" % i, [128, 514], F32) for i in range(2)]
        accgs = [sb("accg%d" % i, [128, 512], F32) for i in range(2)]; accus = [sb("accu%d" % i, [128, 512], F32) for i in range(2)]
        ug = ugs[0]; uu = uus[0]; accg = accgs[0]; accu = accus[0]
        uh = sb("uh", [128, 44, 2], F32); ulast = sb("ulast", [128, 44, 2], F32)
        rden = sb("rden", [128, 512], F32)
        ss = sb("ss", [128, 4], F32); rstd = sb("rstd", [128, 4], F32); epsT = sb("epsT", [128, 1], F32)
        stage = sb("stage", [128, 1280], F32)
        ones32 = stage[:, 1024:1152]
        cst = stage[:, 0:1024].bitcast(BF16).rearrange("p (a c) -> p a c", c=512)
        wbuf = sb("wbuf", [128, 5, 4096], BF16)
        exb = stage[0:8, 0:512]
        psum = pst_("psum", [128, 4096], F32)

        def bank(i):
            return psum[:, i * 512:(i + 1) * 512]
        psS = [bank(5), bank(6), bank(7)]
        PSK = [('pb', 5), ('pb', 6), ('pb', 7)]

        qT = big[:, 0:4096].rearrange("p (a n) -> p a n", n=512)
        oT = big[:, 4096:8192].rearrange("p (a n) -> p a n", n=512)
        mT = big[:, 8192:12288].rearrange("p (a n) -> p a n", n=512)
        m2T = big[:, 0:11264].rearrange("p (a n) -> p a n", n=512)
        BIG = [('big', r_, t_) for r_ in range(3) for t_ in range(4)]

        def BK(r_, ts):
            return [('big', r_, t_) for t_ in ts]

        def m2key(j):
            return BK((j * 512) // 4096, range(4))

        def MM(out, lhsT, rhs, st, sp_, R, W):
            S.op('pe', lambda e: e.matmul(out, lhsT=lhsT, rhs=rhs, start=st, stop=sp_), R, W)

        def TR(out, in_, ident, R, W):
            S.op('pe', lambda e: e.transpose(out, in_, ident), R, W)

        def ACT(out, in_, func, R, W, bias=None, scale=None, accum=None):
            kw = {}
            if bias is not None:
                kw['bias'] = bias
            if scale is not None:
                kw['scale'] = scale
            if accum is not None:
                kw['accum_out'] = accum
            S.op('act', lambda e: e.activation(out=out, in_=in_, func=func, **kw), R, W)

        def TS(eng, out, in0, s1, s2, op0, op1, R, W):
            if s2 is None:
                S.op(eng, lambda e: e.tensor_scalar(out=out, in0=in0, scalar1=s1, scalar2=None, op0=op0), R, W)
            else:
                S.op(eng, lambda e: e.tensor_scalar(out=out, in0=in0, scalar1=s1, scalar2=s2, op0=op0, op1=op1), R, W)

        def STT(eng, out, in0, sc, in1, op0, op1, R, W):
            S.op(eng, lambda e: e.scalar_tensor_tensor(out=out, in0=in0, scalar=sc, in1=in1, op0=op0, op1=op1), R, W)

        def TT(eng, out, in0, in1, op, R, W):
            S.op(eng, lambda e: e.tensor_tensor(out=out, in0=in0, in1=in1, op=op), R, W)

        def CP(eng, out, in_, R, W):
            if eng == 'act':
                S.op('act', lambda e: e.copy(out=out, in_=in_), R, W)
            else:
                S.op(eng, lambda e: e.tensor_copy(out=out, in_=in_), R, W)

        def SCL(eng, out, in_, factor, R, W):
            if eng == 'act':
                S.op('act', lambda e: e.mul(out=out, in_=in_, mul=factor), R, W)
            else:
                TS(eng, out, in_, factor, None, ALU.mult, None, R, W)

        def DMA(q, out, in_, R, W, slow=False):
            if slow:
                S.op(q, lambda e: e.dma_start(out=out, in_=in_, allow_slow_non_contiguous=True), R, W, dma=True)
            else:
                S.op(q, lambda e: e.dma_start(out=out, in_=in_), R, W, dma=True)

        rot = {'d': 0, 'w': 0, 'tr': 0, 'pt': 0, 'ptx': 0, 'cp': 0}

        def nbank():
            i = rot['d'] % 3
            rot['d'] += 1
            return bank(i), ('pb', i)

        def ntr():
            i = rot['tr'] % 2
            rot['tr'] += 1
            return bank(2 + i).bitcast(BF16), ('pb', 2 + i)

        def cpeng():
            rot['cp'] += 1
            return 'dve' if rot['cp'] % 2 else 'act'

        def wload(pieces, kc, width, krow0=0):
            bi = rot['w'] % 5
            rot['w'] += 1
            view = wbuf[:, bi, 0:kc * width].rearrange("p (k c) -> p k c", c=width)
            for (scr, l, c0, ncols, dcol) in pieces:
                src = scr.ap()[l, krow0 * 128:(krow0 + kc) * 128, c0:c0 + ncols].rearrange("(k p) c -> p k c", p=128)
                DMA('sp', view[:, :, dcol:dcol + ncols], src, PK[(l, scr.name)], [('wbuf', bi)])
            return view, ('wbuf', bi)

        DMA('pool', cI[:], c_I.ap()[:, :], [], ['cI'])
        DMA('pool', cJ[:], c_J.ap()[:, :], [], ['cJ'])
        DMA('pool', cJ16[:], c_J16.ap()[:, :], [], ['cJ16'])
        DMA('pool', cA[:].rearrange("p a b c -> p (a b c)"), c_A.ap()[:, :], [], ['cA'])
        DMA('pool', cAs[:].rearrange("p a b c -> p (a b c)"), c_As.ap()[:, :], [], ['cAs'])
        DMA('pool', cMB[:].rearrange("p a b -> p (a b)"), c_MB.ap()[:, :], [], ['cMB'])
        DMA('sp', vblk[:], c_vblk.ap()[:, :], [], ['vblk'])
        DMA('sp', esink[:], sink_l.ap()[:, :], [], ['esink'])
        DMA('sp', wcv[:], wconv_l.ap()[:, :], [], ['wcv'])
        DMA('sp', bcv[:], bconv_l.ap()[:, :], [], ['bcv'])
        S.op('dve', lambda e: e.memset(ones32[:], 1.0), [], ['stage'])
        S.op('dve', lambda e: e.memset(epsT[:], EPS), [], ['epsT'])
        CP('dve', ones[:], ones32[:], ['stage'], ['ones'])
        ACT(esink[:], esink[:], AF.Exp, ['esink'], ['esink'])
        DMA('pool', negrow[:, :], c_negrow.ap()[:, :], [], ['negrow'])

        def vones_ap(gb):
            return ones[:], 'ones'

        PK = {}

        def prep_list(l):
            P = []

            def cast(dst, dc0, src, sc0, ncols, nrows):
                for r0 in range(0, nrows, 1024):
                    r1 = min(nrows, r0 + 1024)
                    for c in range(0, ncols, 512):
                        w_ = min(512, ncols - c)
                        key = ('wp', l, dst.name, r0, dc0 + c)
                        PK.setdefault((l, dst.name), []).append(key)
                        P.append((dst.ap()[l, r0:r1, dc0 + c:dc0 + c + w_], src.ap()[l, r0:r1, sc0 + c:sc0 + c + w_], key))
            cast(s_wxkv, 0, w_xk, 0, 512, D); cast(s_wxkv, 512, w_xv, 0, 512, D)
            cast(s_wk, 0, w_mix_in, 1280, 512, D)
            for i in range(4):
                cast(s_wk, 512 + 64 * i, w_mix_in, 512 + 64 * (i // 2), 64, D)
                cast(s_wv, 512 + 64 * i, w_mix_in, 640 + 64 * (i // 2), 64, D)
            cast(s_wv, 0, w_mix_in, 1792, 512, D)
            key = ('wp', l, 'ext')
            PK.setdefault((l, 'ext'), []).append(key)
            P.append((ext.ap()[l, :, 383:767], rel.ap()[l, :, 0:384], key))
            cast(s_wq, 0, w_mix_in, 0, 512, D); cast(s_wq, 512, w_mix_in, 768, 512, D)
            cast(s_wkvo, 0, w_mix_in, 512, 256, D); cast(s_wkvo, 256, w_mix_in, 1280, 1024, D)
            cast(s_wg, 0, w_mix_in, 2304, 2048, D)
            cast(s_woa, 0, w_o_a, 0, D, 512); cast(s_wob, 0, w_o_b, 0, D, 512); cast(s_wout, 0, w_mix_out, 0, D, D)
            cast(s_wxq, 0, w_xq, 0, 512, D)
            cast(s_wxo, 0, w_xo, 0, D, 512)
            cast(s_wup, 0, w_up, 0, 2 * DFF, D); cast(s_wdn, 0, w_down, 0, D, DFF)
            return P

        def do_prep(l):
            for (dst, src, key) in prep_list(l):
                DMA('pool', dst, src, [], [key], slow=True)

        def norm_to_hT(l_g, gi, nb, bs, blocks=None):
            blks = list(blocks if blocks is not None else range(nb))
            for t in blks:
                ACT(hn[:bs, t, :], xts[CUR['xb']][:bs, t, :], AF.Square, [('xt', CUR['xb'], t)], [('hn', t), ('ss', t)], accum=ss[:bs, t:t + 1])
                ACT(rstd[:bs, t:t + 1], ss[:bs, t:t + 1], AF.Sqrt, [('ss', t), 'epsT'], [('rstd', t)], bias=epsT[:bs, :], scale=1.0 / D)
            for t in blks:
                S.op('dve', (lambda t=t: (lambda e: e.reciprocal(out=rstd[:bs, t:t + 1], in_=rstd[:bs, t:t + 1])))(), [('rstd', t)], [('rstd', t)])
                STT('dve', hn[:bs, t, :], xts[CUR['xb']][:bs, t, :], rstd[:bs, t:t + 1], g_bc[:bs, gi, :], ALU.mult, ALU.mult,
                    [('xt', CUR['xb'], t), ('rstd', t), ('g', gi)], [('hn', t)])
            for t in blks:
                pt_, pk = ntr()
                for kc in range(8):
                    TR(pt_[:, kc * 128:kc * 128 + bs], hn[:bs, t, kc * 128:(kc + 1) * 128], cI[:bs, :bs], [('hn', t), 'cI'], [pk])
                CP(cpeng(), hT[:, :, t * 128:t * 128 + bs], pt_[:, :].rearrange("p (k c) -> p k c", c=128)[:, :, 0:bs], [pk], [('hT', t)])

        def hT_keys(nb):
            return [('hT', t) for t in range(nb)]

        def dense_fm(wview, wkey, kc_n, col0, rhs_fn, rkeys, n, evac):
            pb, pk = nbank()
            for kc in range(kc_n):
                MM(pb[:, 0:n], wview[:, kc, col0:col0 + 128], rhs_fn(kc), kc == 0, kc == kc_n - 1, [wkey] + rkeys, [pk])
            evac(pb, pk)

        def dense_tm(lhs_fn, lkeys, kc_n, wview, wkey, col0, ncols, bs, evac):
            pb, pk = nbank()
            for kc in range(kc_n):
                MM(pb[:bs, 0:ncols], lhs_fn(kc), wview[:, kc, col0:col0 + ncols], kc == 0, kc == kc_n - 1, [wkey] + lkeys, [pk])
            evac(pb, pk)

        def ring_runs(blk0, nb):
            runs = []
            t = 0
            while t < nb:
                s0 = (blk0 + t) % 8
                c = min(nb - t, 8 - s0)
                runs.append((t, c, s0))
                t += c
            return runs

        def attention(l, t, nq, qcol, kblocks_b, kblocks_a, sample):
            for kind in ('a', 'b'):
                kbl = kblocks_a if kind == 'a' else kblocks_b
                nkb = len(kbl)
                for j in range(4):
                    pi = rot['pt'] % 2
                    rot['pt'] += 1
                    P_ = PT[pi]
                    pkey = ('PT', pi)
                    qa = j if kind == 'a' else 4 + j
                    for e in range(2):
                        h = 2 * j + e
                        sb_, sk = (psS[0], PSK[0]) if e == 0 else (psS[2], PSK[2])
                        for jj, (slot, nk, _, _) in enumerate(kbl):
                            if jj < 4:
                                out = sb_[:nk, jj * 128:jj * 128 + nq]
                                ok = sk
                            else:
                                out = psS[1][:nk, e * 128:e * 128 + nq]
                                ok = PSK[1]
                            if kind == 'a':
                                lk = kTa[64 * e:64 * e + 64, j // 2, slot * 128:slot * 128 + nk]
                                kk = ('kTa', slot)
                            else:
                                lk = kTb[64 * e:64 * e + 64, j, slot * 128:slot * 128 + nk]
                                kk = ('kTb', slot)
                            MM(out, lk, qT[64 * e:64 * e + 64, qa, qcol:qcol + nq], True, False, [kk] + BK(0, range(4)), [ok])
                            if kind == 'a':
                                rb = cAs[:nk, jj, h, :] if sample else cA[:nk, jj, h, :]
                                MM(out, cI[:nk, :nk], rb, False, True, ['cI', 'cA', 'cAs'], [ok])
                            else:
                                if sample:
                                    MM(out, hank[0:16, h, jj * 128:jj * 128 + nk], cJ16[:, :], False, True, ['hank', 'cJ16'], [ok])
                                else:
                                    msk = jj in (0, 4)
                                    MM(out, hank[:, h, jj * 128:jj * 128 + nk], cJ[:, :], False, not msk, ['hank', 'cJ'], [ok])
                                    if msk:
                                        MM(out, cI[:, :], cMB[:, 0 if jj == 0 else 1, :], False, True, ['cI', 'cMB'], [ok])
                        nmain = min(nkb, 4)
                        if sample:
                            for jj, (slot, nk, _, _) in enumerate(kbl):
                                if jj < 4:
                                    ACT(P_[:nk, jj, e * 128:e * 128 + nq], sb_[:nk, jj * 128:jj * 128 + nq], AF.Exp, [sk], [pkey])
                                else:
                                    ACT(P_[:nk, jj, e * 128:e * 128 + nq], psS[1][:nk, e * 128:e * 128 + nq], AF.Exp, [PSK[1]], [pkey])
                        else:
                            ACT(P_[:, 0:nmain, e * 128:(e + 1) * 128], sb_[:, 0:nmain * 128].rearrange("p (a b) -> p a b", b=128),
                                AF.Exp, [sk], [pkey])
                            if nkb > 4:
                                ACT(P_[:, 4, e * 128:(e + 1) * 128], psS[1][:, e * 128:(e + 1) * 128], AF.Exp, [PSK[1]], [pkey])
                    pb, pk = nbank()
                    W2 = 128 + nq
                    for jj, (slot, nk, _, _) in enumerate(kbl):
                        if kind == 'a':
                            lv = va[:nk, slot, (j // 2) * 128:(j // 2) * 128 + 128]
                            vk = ('va', slot)
                        else:
                            lv = vb[:nk, slot, j * 128:(j + 1) * 128]
                            vk = ('vb', slot)
                        MM(pb[:, 0:W2], lv, P_[:nk, jj, 0:W2], jj == 0, jj == nkb - 1, [vk, pkey], [pk])
                    for jj, (slot, nk, vo, vokey) in enumerate(kbl):
                        MM(pb[:, 256:256 + W2], vo[:nk, :], P_[:nk, jj, 0:W2], jj == 0, jj == nkb - 1, [vokey, pkey], [pk])
                    if kind == 'a':
                        TS('dve', rden[:, 0:W2], pb[:, 256:256 + W2], esink[:, l * 4 + j:l * 4 + j + 1], None, ALU.add, None,
                           [pk, 'esink'], ['rden'])
                    else:
                        TS('dve', rden[:, 0:W2], pb[:, 256:256 + W2], 1e-30, None, ALU.add, None, [pk], ['rden'])
                    S.op('dve', (lambda W2=W2: (lambda e_: e_.reciprocal(out=rden[:, 0:W2], in_=rden[:, 0:W2])))(), ['rden'], ['rden'])
                    oi = j if kind == 'a' else 4 + j
                    TT('dve', oT[0:64, oi, qcol:qcol + nq], pb[0:64, 0:nq], rden[0:64, 0:nq], ALU.mult, [pk, 'rden'], BK(1, range(4)))
                    TT('dve', oT[64:128, oi, qcol:qcol + nq], pb[64:128, 128:128 + nq], rden[64:128, 128:128 + nq], ALU.mult,
                       [pk, 'rden'], BK(1, range(4)))


        SETK = [[('pb', 3), ('pb', 4), ('pb', 5)], [('pb', 6), ('pb', 7), ('pb', 5)]]
        MAINB = [3 * 512, 6 * 512]
        J4B = [5 * 512, 5 * 512 + 256]

        def att_units(l, blk0, nb):
            units = []
            for t in range(nb):
                gb = blk0 + t
                for kind in ('a', 'b'):
                    if kind == 'a':
                        kbl = [((gb - 1 + i) % 8, gb - 1 + i) for i in range(2)]
                    else:
                        kbl = [((gb - 4 + i) % 8, gb - 4 + i) for i in range(5)]
                    for j in range(4):
                        units.append({'l': l, 'qcol': t * 128, 'kind': kind, 'j': j, 'kbl': kbl})
            for i, u in enumerate(units):
                u['set'] = i % 2
                u['pi'] = i % 2
            return units

        def u_col(u, e, jj):
            nkb = len(u['kbl'])
            nj = min(nkb, 4)
            if jj < nj:
                jp = (3 - jj) if u['kind'] == 'b' else jj
                return MAINB[u['set']] + e * nj * 128 + jp * 128
            return J4B[u['set']] + e * 128

        def att_S(u):
            kind, j, qcol = u['kind'], u['j'], u['qcol']
            qa = j if kind == 'a' else 4 + j
            base = MAINB[u['set']]
            nkb = len(u['kbl'])

            def emit(group):
                for k_, (out, lt, rh, rk, ok) in enumerate(group):
                    MM(out, lt, rh, k_ == 0, k_ == len(group) - 1, rk, [ok])

            def qk_ops(e, jj):
                slot, blk = u['kbl'][jj]
                h = 2 * j + e
                c0 = u_col(u, e, jj)
                out = psum[:, c0:c0 + 128]
                ok = ('pb', c0 // 512)
                ops_ = []
                if kind == 'a':
                    ops_.append((out, kTa[64 * e:64 * e + 64, j // 2, slot * 128:slot * 128 + 128],
                                 qT[64 * e:64 * e + 64, qa, qcol:qcol + 128], [('kTa', slot), ('big', 0, qcol // 128)], ok))
                else:
                    ops_.append((out, kTb[64 * e:64 * e + 64, j, slot * 128:slot * 128 + 128],
                                 qT[64 * e:64 * e + 64, qa, qcol:qcol + 128], [('kTb', slot), ('big', 0, qcol // 128)], ok))
                if blk < NVON and not (kind == 'b' and jj < 4):
                    kq = (NVON + 2 - blk) * 128
                    ops_.append((out, negrow[0:1, kq:kq + 128], ones[0:1, :], ['negrow', 'ones'], ok))
                return ops_
            for e in range(2):
                h = 2 * j + e
                if kind == 'a':
                    c0 = base + e * 256
                    grp = [(psum[:, c0:c0 + 256], cI[:, :], cA[:, h, :, :].rearrange("p a b -> p (a b)"), ['cI', 'cA'], ('pb', c0 // 512))]
                    for jj in range(nkb):
                        grp += qk_ops(e, jj)
                    emit(grp)
                else:
                    c0 = base + e * 512
                    grp = [(psum[:, c0:c0 + 512], cJ[:, :], hank[:, h, 128:640], ['hank', 'cJ'], ('pb', c0 // 512))]
                    blk3 = u['kbl'][3][1]
                    if blk3 - 3 < NVON:
                        k0 = (NVON + 2 - blk3) * 128
                        grp.append((psum[:, c0:c0 + 512], ones[0:1, 0:128], negrow[0:1, k0:k0 + 512], ['negrow', 'ones'], ('pb', c0 // 512)))
                    for jj in range(4):
                        grp += qk_ops(e, jj)
                    emit(grp)
                    c1 = J4B[u['set']] + e * 128
                    grp = [(psum[:, c1:c1 + 128], cJ[:, :], hank[:, h, 0:128], ['hank', 'cJ'], ('pb', c1 // 512))]
                    grp += qk_ops(e, 4)
                    emit(grp)

        def att_EXP(u):
            nkb = len(u['kbl'])
            nj = min(nkb, 4)
            mb = MAINB[u['set']]
            Wm = 2 * nj * 128
            ACT(PTf[u['pi']][:, 0:Wm], psum[:, mb:mb + Wm], AF.Exp, [('pb', mb // 512), ('pb', (mb + Wm - 1) // 512)], [('PT', u['pi'])])
            if nkb > 4:
                jb = J4B[u['set']]
                ACT(PTf[u['pi']][:, Wm:Wm + 256], psum[:, jb:jb + 256], AF.Exp, [('pb', 5)], [('PT', u['pi'])])

        def att_PV(u):
            kind, j = u['kind'], u['j']
            nkb = len(u['kbl'])
            nj = min(nkb, 4)
            P_ = PTf[u['pi']]
            pkey = ('PT', u['pi'])
            pb, pk = nbank()

            def rhs_out(jj, c_out):
                if jj < nj:
                    jp = (3 - jj) if kind == 'b' else jj
                    r = P_[:, 0:2 * nj * 128].rearrange("p (e j q) -> p e j q", e=2, j=nj)[:, :, jp, :]
                    o = pb[:, c_out:c_out + 256].rearrange("p (e q) -> p e q", e=2)
                else:
                    r = P_[:, 2 * nj * 128:2 * nj * 128 + 256]
                    o = pb[:, c_out:c_out + 256]
                return r, o
            for jj, (slot, blk) in enumerate(u['kbl']):
                if kind == 'a':
                    lv = va[:, slot, (j // 2) * 128:(j // 2) * 128 + 128]
                    vk = ('va', slot)
                else:
                    lv = vb[:, slot, j * 128:(j + 1) * 128]
                    vk = ('vb', slot)
                r, o = rhs_out(jj, 0)
                MM(o, lv, r, jj == 0, jj == nkb - 1, [vk, pkey], [pk])
            for jj in range(nkb):
                r, o = rhs_out(jj, 256)
                MM(o, ones[:, :], r, jj == 0, jj == nkb - 1, ['ones', pkey], [pk])
            u['pb'] = (pb, pk)

        def att_NORM(u):
            kind, j, qcol, l = u['kind'], u['j'], u['qcol'], u['l']
            pb, pk = u['pb']
            for (r0, c0) in ((0, 256), (64, 384)):
                if kind == 'a':
                    TS('dve', rden[r0:r0 + 64, 0:128], pb[r0:r0 + 64, c0:c0 + 128], esink[r0:r0 + 64, l * 4 + j:l * 4 + j + 1], None,
                       ALU.add, None, [pk, 'esink'], ['rden'])
                else:
                    TS('dve', rden[r0:r0 + 64, 0:128], pb[r0:r0 + 64, c0:c0 + 128], 1e-30, None, ALU.add, None, [pk], ['rden'])
            S.op('dve', lambda e_: e_.reciprocal(out=rden[:, 0:128], in_=rden[:, 0:128]), ['rden'], ['rden'])
            oi = j if kind == 'a' else 4 + j
            TT('dve', oT[0:64, oi, qcol:qcol + 128], pb[0:64, 0:128], rden[0:64, 0:128], ALU.mult, [pk, 'rden'], [('big', 1, qcol // 128)])
            TT('dve', oT[64:128, oi, qcol:qcol + 128], pb[64:128, 128:256], rden[64:128, 0:128], ALU.mult, [pk, 'rden'], [('big', 1, qcol // 128)])

        def attention_prompt(l, blk0, nb):
            units = att_units(l, blk0, nb)
            att_S(units[0])
            for i, u in enumerate(units):
                if i + 1 < len(units):
                    att_S(units[i + 1])
                att_EXP(u)
                att_PV(u)
                att_NORM(u)

        def load_g(l, which, gi):
            src = {'mix': g_mix, 'x': g_xattn, 'ffn': g_ffn, 'mem': g_mem}[which]
            DMA('sp', g_bc[:, gi, :], bass.AP(src, l * D, [[0, 128], [1, D]]), [], [('g', gi)])

        def kv_project(l, blk0, nb, bs, sample):
            n = nb * bs if not sample else bs
            hk = hT_keys(nb)
            wv_, wk_ = wload([(s_wk, l, 0, 512, 0)], 8, 512)
            for f in range(4):
                def ev(pb, pk, f=f):
                    if sample:
                        CP(cpeng(), kTb[:, f, 4 * 128:4 * 128 + bs], pb[:, 0:bs], [pk], [('kTb', 4)])
                    else:
                        for (t0, c, s0) in ring_runs(blk0, nb):
                            CP(cpeng(), kTb[:, f, s0 * 128:(s0 + c) * 128], pb[:, t0 * 128:(t0 + c) * 128], [pk],
                               [('kTb', s0 + i) for i in range(c)])
                dense_fm(wv_, wk_, 8, f * 128, lambda kc: hT[:, kc, 0:n], hk, n, ev)
            wv_, wk_ = wload([(s_wk, l, 512, 256, 0)], 8, 256)
            for g in range(2):
                def ev(pb, pk, g=g):
                    if sample:
                        CP(cpeng(), kTa[:, g, 4 * 128:4 * 128 + bs], pb[:, 0:bs], [pk], [('kTa', 4)])
                    else:
                        for (t0, c, s0) in ring_runs(blk0, nb):
                            CP(cpeng(), kTa[:, g, s0 * 128:(s0 + c) * 128], pb[:, t0 * 128:(t0 + c) * 128], [pk],
                               [('kTa', s0 + i) for i in range(c)])
                dense_fm(wv_, wk_, 8, g * 128, lambda kc: hT[:, kc, 0:n], hk, n, ev)
            wv_, wk_ = wload([(s_wv, l, 0, 512, 0)], 8, 512)
            wv2, wk2 = wload([(s_wv, l, 512, 256, 0)], 8, 256)
            for t in range(nb):
                slot = 4 if sample else (blk0 + t) % 8
                gb = blk0 + t

                def evb(pb, pk, slot=slot, gb=gb):
                    if sample:
                        CP(cpeng(), vb[:bs, slot, :], pb[:bs, 0:512], [pk], [('vb', slot)])
                    else:
                        TS('dve', vb[:, slot, :], pb[:, 0:512], vblk[:, gb:gb + 1], None, ALU.mult, None, [pk, 'vblk'], [('vb', slot)])

                def eva(pb, pk, slot=slot, gb=gb):
                    if sample:
                        CP(cpeng(), va[:bs, slot, :], pb[:bs, 0:256], [pk], [('va', slot)])
                    else:
                        TS('dve', va[:, slot, :], pb[:, 0:256], vblk[:, gb:gb + 1], None, ALU.mult, None, [pk, 'vblk'], [('va', slot)])
                dense_tm(lambda kc, t=t: hT[:, kc, t * 128:t * 128 + bs], [('hT', t)], 8, wv_, wk_, 0, 512, bs, evb)
                dense_tm(lambda kc, t=t: hT[:, kc, t * 128:t * 128 + bs], [('hT', t)], 8, wv2, wk2, 0, 256, bs, eva)

        def kv_outputs(l, blk0, nb, bs, sample):
            outb = [t for t in range(nb) if sample or blk0 + t >= 32]
            if not outb:
                return
            groups = [(0, 512), (512, 512), (1024, 256)]
            wl = [wload([(s_wkvo, l, c0, nc_, 0)], 8, nc_) for (c0, nc_) in groups]
            for t in outb:
                for gi_, (c0, nc_) in enumerate(groups):
                    def ev(pb, pk, c0=c0, nc_=nc_):
                        CP(cpeng(), stage[:bs, c0:c0 + nc_], pb[:bs, 0:nc_], [pk], ['stage'])
                    dense_tm(lambda kc, t=t: hT[:, kc, t * 128:t * 128 + bs], [('hT', t)], 8, wl[gi_][0], wl[gi_][1], 0, nc_, bs, ev)
                if sample:
                    DMA('pool', o_skv.ap()[l, :, :], stage[:bs, :], ['stage'], [('o_skv', l)])
                else:
                    DMA('pool', o_pkv.ap()[l, blk0 + t - 32, :, :], stage[:, :], ['stage'], [('o_pkv', l, blk0 + t)])

        def load_x(l, blk0, nb, bs, sample):
            XB['i'] ^= 1; CUR['xb'] = XB['i']
            for t in range(nb):
                if sample:
                    src = xs_in.ap()[:, :] if l == 0 else xssc.ap()[:, :]
                    DMA('sp', xts[CUR['xb']][:bs, t, :], src, ['xssc'], [('xt', CUR['xb'], t)])
                else:
                    gb = blk0 + t
                    src = (xin if l == 0 else xsc).ap()[gb * 128:(gb + 1) * 128, :]
                    DMA('sp', xts[CUR['xb']][:, t, :], src, [('xsc', gb)], [('xt', CUR['xb'], t)])

        def tile_kv(l, blk0, nb):
            load_x(l, blk0, nb, 128, False)
            norm_to_hT(l, 0, nb, 128)
            kv_project(l, blk0, nb, 128, False)

        def sample_prep(l):
            DMA('pool', cst[:, :, :], cbk.ap()[l].rearrange("(a p) c -> p a c", p=128), [], ['stage'])
            for a in range(4):
                pt_, pk = ntr()
                for f in range(4):
                    TR(pt_[:, f * 128:(f + 1) * 128], cst[:, a, f * 128:(f + 1) * 128], cI[:, :], ['stage', 'cI'], [pk])
                CP(cpeng(), kTb[:, :, a * 128:(a + 1) * 128], pt_[:, 0:512].rearrange("p (f c) -> p f c", c=128), [pk], [('kTb', a)])
            DMA('pool', vb[:, 0:4, :], cbv.ap()[l].rearrange("(a p) c -> p a c", p=128), [], [('vb', i) for i in range(4)])
            cview = cst[:, 0, 0:256].rearrange("p (g d c) -> p g d c", g=2, d=2)
            for d_ in range(2):
                DMA('pool', cview[:, :, d_, :], cak.ap()[l].rearrange("p (g c) -> p g c", g=2), [], ['stage'])
            pt_, pk = ntr()
            for g in range(2):
                TR(pt_[:, g * 128:(g + 1) * 128], cst[:, 0, g * 128:(g + 1) * 128], cI[:, :], ['stage', 'cI'], [pk])
            CP(cpeng(), kTa[:, :, 3 * 128:4 * 128], pt_[:, 0:256].rearrange("p (g c) -> p g c", c=128), [pk], [('kTa', 3)])
            vview = va[:, 3, :].rearrange("p (g d c) -> p g d c", g=2, d=2)
            for d_ in range(2):
                DMA('pool', vview[:, :, d_, :], cav.ap()[l].rearrange("p (g c) -> p g c", g=2), [], [('va', 3)])
            DMA('pool', cst[:, 0:2, :], cmk.ap()[l].rearrange("(a p) c -> p a c", p=128), [], ['stage'])
            for a in range(2):
                pt_, pk = ntr()
                for h in range(4):
                    TR(pt_[:, h * 128:(h + 1) * 128], cst[:, a, h * 128:(h + 1) * 128], cI[:, :], ['stage', 'cI'], [pk])
                CP(cpeng(), mkT[:, :, a * 128:(a + 1) * 128], pt_[:, 0:512].rearrange("p (f c) -> p f c", c=128), [pk], ['mkT'])
            DMA('pool', mv[:, :, :], cmv.ap()[l].rearrange("(a p) c -> p a c", p=128), [], ['mv'])
            DMA('sp', uh[:, :, :], sconv_in.ap()[l], [], ['uh'])
            DMA('pool', hank[0:16, :, 0:528], bass.AP(ext, l * 8 * 768 + 112, [[1, 16], [768, 8], [1, 528]]), PK[(l, 'ext')] + [('extc', l)], ['hank'], slow=True)

        def mem_kv(l):
            sub = dbg[2] if (dbg is not None and dbg[1] == 1) else 99
            load_g(l, 'mem', 0)
            if sub < 1:
                return
            XB['i'] ^= 1; CUR['xb'] = XB['i']
            for t in range(2):
                DMA('sp', xts[CUR['xb']][:, t, :], mem_in.ap()[t * 128:(t + 1) * 128, :], [], [('xt', CUR['xb'], t)])
            if sub < 2:
                return
            norm_to_hT(l, 0, 2, 128)
            if sub < 3:
                return
            hk = hT_keys(2)
            w1, k1 = wload([(s_wxkv, l, 0, 512, 0)], 8, 512)
            w2, k2 = wload([(s_wxkv, l, 512, 512, 0)], 8, 512)
            if sub < 4:
                return
            for h in range(4):
                def ev(pb, pk, h=h):
                    CP(cpeng(), mkT[:, h, :], pb[:, 0:256], [pk], ['mkT'])
                dense_fm(w1, k1, 8, h * 128, lambda kc: hT[:, kc, 0:256], hk, 256, ev)
            if sub < 5:
                return
            for t in range(2):
                def evk(pb, pk):
                    CP(cpeng(), stage[:, 0:512], pb[:, 0:512], [pk], ['stage'])

                def evv(pb, pk, t=t):
                    import os
                    v_ = os.environ.get('EVV', 'ab')
                    if 'a' in v_:
                        CP('act', stage[:, 512:1024], pb[:, 0:512], [pk], ['stage'])
                    if 'b' in v_:
                        CP('dve', mv[:, t, :], pb[:, 0:512], [pk], ['mv'])
                    if 'c' in v_:
                        CP('dve', stage[:, 512:1024], pb[:, 0:512], [pk], ['stage'])
                dense_tm(lambda kc, t=t: hT[:, kc, t * 128:(t + 1) * 128], [('hT', t)], 8, w1, k1, 0, 512, 128, evk)
                if sub < 6:
                    return
                dense_tm(lambda kc, t=t: hT[:, kc, t * 128:(t + 1) * 128], [('hT', t)], 8, w2, k2, 0, 512, 128, evv)
                if sub < 7:
                    return
                DMA('pool', o_pm.ap()[l, t * 128:(t + 1) * 128, :], stage[:, 0:1024], ['stage'], [('o_pm', l, t)])
                if sub < 8:
                    return

        def tile_head(l, blk0, nb, sample=False):
            bs = 16 if sample else 128
            load_x(l, blk0, nb, bs, sample)
            norm_to_hT(l, 0, nb, bs)
            return CUR['xb']

        def tile_full(l, blk0, nb, xb, sample=False, last=False):
            bs = 16 if sample else 128
            n = bs if sample else nb * 128
            hk = hT_keys(nb)
            CUR['xb'] = xb
            for half in range(2):
                wv_, wk_ = wload([(s_wq, l, half * 512, 512, 0)], 8, 512)
                for f in range(4):
                    def ev(pb, pk, f=f, half=half):
                        SCL('dve' if f % 2 else 'act', qT[:, half * 4 + f, 0:n], pb[:, 0:n], 0.125, [pk], BK(0, range(4)))
                    dense_fm(wv_, wk_, 8, f * 128, lambda kc: hT[:, kc, 0:n], hk, n, ev)
            kv_project(l, blk0, nb, bs, sample)
            kv_outputs(l, blk0, nb, bs, sample)
            if sample:
                kb_b = [(a, 128, ones, 'ones') for a in range(4)] + [(4, 16, ones, 'ones')]
                kb_a = [(3, 128, ones, 'ones'), (4, 16, ones, 'ones')]
                attention(l, 0, 16, 0, kb_b, kb_a, True)
            else:
                attention_prompt(l, blk0, nb)
            for half in range(2):
                wga, kga = wload([(s_wg, l, half * 512, 512, 0)], 8, 512)
                wgb, kgb = wload([(s_wg, l, 1024 + half * 512, 512, 0)], 8, 512)
                wo, ko = wload([(s_woa, l, half * 512, 512, 0), (s_wob, l, half * 512, 512, 512)], 4, 1024)
                for f in range(4):
                    fo = half * 4 + f

                    def ev_sa(pb, pk):
                        ACT(accg[:, 0:n], pb[:, 0:n], AF.Sigmoid, [pk], [('accg', 0)])

                    def ev_pa(pb, pk):
                        TT('dve', ug[:, 0:n], pb[:, 0:n], accg[:, 0:n], ALU.mult, [pk, ('accg', 0)], [('ug', 0)])

                    def ev_sb(pb, pk):
                        ACT(accu[:, 0:n], pb[:, 0:n], AF.Sigmoid, [pk], [('accu', 0)])

                    def ev_pb(pb, pk, fo=fo):
                        TT('dve', uu[:, 0:n], pb[:, 0:n], accu[:, 0:n], ALU.mult, [pk, ('accu', 0)], [('uu', 0)])
                        TT('dve', mT[:, fo, 0:n], ug[:, 0:n], uu[:, 0:n], ALU.add, [('ug', 0), ('uu', 0)], BK(2, range(4)))
                    dense_fm(wga, kga, 8, f * 128, lambda kc: hT[:, kc, 0:n], hk, n, ev_sa)
                    dense_fm(wo, ko, 4, f * 128, lambda kc: oT[:, kc, 0:n], BK(1, range(4)), n, ev_pa)
                    dense_fm(wgb, kgb, 8, f * 128, lambda kc: hT[:, kc, 0:n], hk, n, ev_sb)
                    dense_fm(wo, ko, 4, 512 + f * 128, lambda kc: oT[:, 4 + kc, 0:n], BK(1, range(4)), n, ev_pb)
            for half in range(2):
                wv_, wk_ = wload([(s_wout, l, half * 512, 512, 0)], 8, 512)
                for t in range(nb):
                    def ev(pb, pk, t=t, half=half):
                        TT('dve', xts[CUR['xb']][:bs, t, half * 512:(half + 1) * 512], pb[:bs, 0:512], xts[CUR['xb']][:bs, t, half * 512:(half + 1) * 512], ALU.add,
                           [pk, ('xt', CUR['xb'], t)], [('xt', CUR['xb'], t)])
                    dense_tm(lambda kc, t=t: mT[:, kc, t * 128:t * 128 + bs], [('big', 2, t)], 8, wv_, wk_, 0, 512, bs, ev)
            halves = [(0, 1)] if (sample or nb == 1) else [(0, nb)]

            def hcols(hf):
                return hf[0] * 128, (hf[0] * 128 + bs) if (sample or nb == 1) else hf[1] * 128
            for hf in halves:
                norm_to_hT(l, 1, nb, bs, blocks=range(hf[0], hf[1]))
            wv_, wk_ = wload([(s_wxq, l, 0, 512, 0)], 8, 512)
            for h in range(4):
                for hf in halves:
                    c0, c1 = hcols(hf)

                    def ev(pb, pk, h=h, c0=c0, c1=c1, hf=hf):
                        TS('dve', qT[:, h, c0:c1], pb[:, 0:c1 - c0], 128.0 ** -0.5, None, ALU.mult, None, [pk], BK(0, range(hf[0], hf[1])))
                    dense_fm(wv_, wk_, 8, h * 128, lambda kc, c0=c0, c1=c1: hT[:, kc, c0:c1], [('hT', t) for t in range(hf[0], hf[1])], c1 - c0, ev)
            SBK = [[(bank(5), ('pb', 5)), (bank(6), ('pb', 6))], [(bank(7), ('pb', 7)), (bank(4), ('pb', 4))]]
            PTxh = [PTx0[:, :, :]] if len(halves) == 1 else [PTx0[:, :, 0:256], PTx0[:, :, 256:512]]
            for h in range(4):
                for hi, hf in enumerate(halves):
                    c0, c1 = hcols(hf)
                    w = c1 - c0
                    for a in range(2):
                        sbk, skk = SBK[hi][a]
                        MM(sbk[:, 0:w], mkT[:, h, a * 128:(a + 1) * 128], qT[:, h, c0:c1], True, True, ['mkT'] + BK(0, range(hf[0], hf[1])), [skk])
                        ACT(PTxh[hi][:, a, 0:w], sbk[:, 0:w], AF.Exp, [skk], [('PTx', hi)])
                for hi, hf in enumerate(halves):
                    c0, c1 = hcols(hf)
                    w = c1 - c0
                    Px = PTxh[hi]
                    pxk = ('PTx', hi)
                    pO, pOk = nbank()
                    pD, pDk = nbank()
                    for a in range(2):
                        MM(pO[:, 0:w], mv[:, a, h * 128:(h + 1) * 128], Px[:, a, 0:w], a == 0, a == 1, ['mv', pxk], [pOk])
                    for a in range(2):
                        MM(pD[:, 0:w], ones[:, :], Px[:, a, 0:w], a == 0, a == 1, ['ones', pxk], [pDk])
                    S.op('dve', (lambda pD=pD, c0=c0, c1=c1, w=w: (lambda e_: e_.reciprocal(out=rden[:, c0:c1], in_=pD[:, 0:w])))(), [pDk], ['rden'])
                    TT('dve', oT[:, h, c0:c1], pO[:, 0:w], rden[:, c0:c1], ALU.mult, [pOk, 'rden'], BK(1, range(hf[0], hf[1])))
            wv_, wk_ = wload([(s_wxo, l, 0, 1024, 0)], 4, 1024)
            for hf in halves:
                for half in range(2):
                    for t in range(hf[0], hf[1]):
                        def ev(pb, pk, t=t, half=half):
                            TT('dve', xts[CUR['xb']][:bs, t, half * 512:(half + 1) * 512], pb[:bs, 0:512], xts[CUR['xb']][:bs, t, half * 512:(half + 1) * 512], ALU.add,
                               [pk, ('xt', CUR['xb'], t)], [('xt', CUR['xb'], t)])
                        dense_tm(lambda kc, t=t: oT[:, kc, t * 128:t * 128 + bs], [('big', 1, t)], 4, wv_, wk_, half * 512, 512, bs, ev)
            for hf in halves:
                norm_to_hT(l, 2, nb, bs, blocks=range(hf[0], hf[1]))
            lastgb = blk0 + nb - 1
            for i in range(11):
                wv_, wk_ = wload([(s_wup, l, 256 * i, 256, 0), (s_wup, l, DFF + 256 * i, 256, 256)], 8, 512)
                for e in range(2):
                    fbk = 2 * i + e
                    bi = fbk % 2
                    for (ub, ac, uk, ak, col0, fidx) in ((ugs[bi], accgs[bi], ('ug', bi), ('accg', bi), e * 128, fbk),
                                                         (uus[bi], accus[bi], ('uu', bi), ('accu', bi), 256 + e * 128, 22 + fbk)):
                        wbase = (l * 3) * 44 + fidx

                        def ev(pb, pk, ub=ub, uk=uk, ac=ac, ak=ak, wbase=wbase, fidx=fidx):
                            CP('act', ub[:, 2:2 + n], pb[:, 0:n], [pk], [uk])
                            ACT(ac[:, 0:n], pb[:, 0:n], AF.Identity, [pk, 'wcv', 'bcv'], [ak],
                                bias=bcv[:, l * 44 + fidx:l * 44 + fidx + 1], scale=wcv[:, wbase + 88:wbase + 89])
                        dense_fm(wv_, wk_, 8, col0, lambda kc: hT[:, kc, 0:n], hk, n, ev)
                        CP('pool', ub[:, 0:2], uh[:, fidx, :], ['uh'], [uk])
                        STT('dve', ac[:, 0:n], ub[:, 1:1 + n], wcv[:, wbase + 44:wbase + 45], ac[:, 0:n], ALU.mult, ALU.add, [uk, ak, 'wcv'], [ak])
                        STT('dve', ac[:, 0:n], ub[:, 0:n], wcv[:, wbase:wbase + 1], ac[:, 0:n], ALU.mult, ALU.add, [uk, ak, 'wcv'], [ak])
                        if last or sample:
                            CP('pool', ulast[:, fidx, :], ub[:, n:n + 2], [uk], ['ulast'])
                        if not sample:
                            TS('pool', uh[:, fidx, :], ub[:, n:n + 2], vblk[:, lastgb:lastgb + 1], None, ALU.mult, None, [uk, 'vblk'], ['uh'])
                    ACT(accgs[bi][:, 0:n], accgs[bi][:, 0:n], AF.Silu, [('accg', bi)], [('accg', bi)])
                    TT('dve', m2T[:, fbk, 0:n], accgs[bi][:, 0:n], accus[bi][:, 0:n], ALU.mult, [('accg', bi), ('accu', bi)], m2key(fbk))
            if last or sample:
                DMA('pool', (o_sconv if sample else o_pconv).ap()[l], ulast[:, :, :], ['ulast'], [('o_conv', sample, l)])
            for ch in range(2):
                groups = [(0, 8), (8, 8), (16, 6)]
                for gi_, (k0, nk_) in enumerate(groups):
                    wv_, wk_ = wload([(s_wdn, l, ch * 512, 512, 0)], nk_, 512, krow0=k0)
                    for t in range(nb):
                        for kc in range(nk_):
                            MM(bank(4 + t)[:bs, 0:512], m2T[:, k0 + kc, t * 128:t * 128 + bs], wv_[:, kc, 0:512],
                               gi_ == 0 and kc == 0, gi_ == 2 and kc == nk_ - 1, [wk_] + BIG, [('pb', 4 + t)])
                if ch == 1:
                    yield
                    CUR['xb'] = xb
                for t in range(nb):
                    TT('dve', xts[CUR['xb']][:bs, t, ch * 512:(ch + 1) * 512], bank(4 + t)[:bs, 0:512], xts[CUR['xb']][:bs, t, ch * 512:(ch + 1) * 512],
                       ALU.add, [('pb', 4 + t), ('xt', CUR['xb'], t)], [('xt', CUR['xb'], t)])
            for t in range(nb):
                if sample:
                    DMA('pool', xssc.ap()[:, :], xts[CUR['xb']][:bs, t, :], [('xt', CUR['xb'], t)], ['xssc'])
                else:
                    gb = blk0 + t
                    DMA('pool', xsc.ap()[gb * 128:(gb + 1) * 128, :], xts[CUR['xb']][:, t, :], [('xt', CUR['xb'], t)], [('xsc', gb)])

        for l in range(DEPTH):
            DMA('sp', exb[:, 384:385], rel.ap()[l, :, 0:1], [], ['stage'], slow=True)
            for c in range(3):
                TS('dve', exb[:, c * 128:(c + 1) * 128], ones32[0:8, :], exb[:, 384:385], None, ALU.mult, None, ['stage', 'stage'], ['stage'])
            DMA('pool', ext.ap()[l, :, 0:383], exb[:, 0:383], ['stage'], [('extc', l)])
            DMA('pool', rext.ap()[l, :, 384:767], exb[:, 0:383], ['stage'], [('extc', l)])
            DMA('pool', rext.ap()[l, :, 0:384], rel_rev.ap()[l, :, 129:513], [], [('extc', l)])
        do_prep(0)
        for l in range(DEPTH if dbg is None else dbg[0]):
            pending = prep_list(l + 1) if l + 1 < DEPTH else []
            ntl = (NBLK - FB[l] + 3) // 4 + 1
            per = (len(pending) + max(1, ntl - 3) - 1) // max(1, ntl - 3) if pending else 0

            def more_prep():
                for _ in range(per):
                    if pending:
                        dst, src, key = pending.pop(0)
                        DMA('pool', dst, src, [], [key], slow=True)
            if dbg is not None and dbg[1] < 1:
                break
            mem_kv(l)
            if dbg is not None and dbg[1] < 2:
                break
            load_g(l, 'mix', 0)
            load_g(l, 'x', 1)
            load_g(l, 'ffn', 2)
            for h in range(8):
                DMA('pool', hank[:, h, :], bass.AP(rext, (l * 8 + h) * 768, [[1, 128], [1, 640]]), PK[(l, 'ext')] + [('extc', l)], ['hank'])
            for h in range(8):
                TT('dve', hank[:, h, 512:640], hank[:, h, 512:640], cMB[:, 0, :], ALU.add, ['hank', 'cMB'], ['hank'])
                TT('dve', hank[:, h, 0:128], hank[:, h, 0:128], cMB[:, 1, :], ALU.add, ['hank', 'cMB'], ['hank'])
            S.op('pool', lambda e: e.memset(uh[:, :, :], 0.0), [], ['uh'])
            fb = FB[l]
            tile_kv(l, fb - 4, 4)
            if dbg is not None and dbg[1] < 3:
                break
            tiles = []
            b0 = fb
            while b0 < NBLK:
                b1 = min(NBLK, (b0 // 4 + 1) * 4)
                tiles.append((b0, b1 - b0, b1 == NBLK))
                b0 = b1
            prev = None
            for (tb0, tnb, tlast) in tiles:
                more_prep()
                xb_ = tile_head(l, tb0, tnb)
                if prev is not None:
                    for _ in prev:
                        pass
                g_ = tile_full(l, tb0, tnb, xb_, last=tlast)
                next(g_)
                prev = g_
            for _ in prev:
                pass
            while pending:
                more_prep()
            sample_prep(l)
            xb_ = tile_head(l, 0, 1, sample=True)
            for _ in tile_full(l, 0, 1, xb_, sample=True):
                pass
        if dbg is None or dbg[1] >= 7:
          DMA('sp', g_bc[:, 0, :], bass.AP(g_final, 0, [[0, 128], [1, D]]), [], [('g', 0)])

        def final_norm(nb, bs, src_fn, dst_fn, key_fn):
            XB['i'] ^= 1; CUR['xb'] = XB['i']
            for t in range(nb):
                DMA('sp', xts[CUR['xb']][:bs, t, :], src_fn(t), [key_fn(t)], [('xt', CUR['xb'], t)])
                ACT(hn[:bs, t, :], xts[CUR['xb']][:bs, t, :], AF.Square, [('xt', CUR['xb'], t)], [('hn', t), ('ss', t)], accum=ss[:bs, t:t + 1])
                ACT(rstd[:bs, t:t + 1], ss[:bs, t:t + 1], AF.Sqrt, [('ss', t), 'epsT'], [('rstd', t)], bias=epsT[:bs, :], scale=1.0 / D)
                S.op('dve', (lambda t=t: (lambda e: e.reciprocal(out=rstd[:bs, t:t + 1], in_=rstd[:bs, t:t + 1])))(), [('rstd', t)], [('rstd', t)])
                STT('dve', stage[:bs, 0:D], xts[CUR['xb']][:bs, t, :], rstd[:bs, t:t + 1], g_bc[:bs, 0, :], ALU.mult, ALU.mult,
                    [('xt', CUR['xb'], t), ('rstd', t), ('g', 0)], ['stage'])
                DMA('pool', dst_fn(t), stage[:bs, 0:D], ['stage'], [('o_y', id(dst_fn), t)])
        for b0 in (range(20, NBLK, 4) if (dbg is None or dbg[1] >= 7) else []):
            final_norm(4, 128,
                       lambda t, b0=b0: xsc.ap()[(b0 + t) * 128:(b0 + t + 1) * 128, :],
                       lambda t, b0=b0: o_y.ap()[(b0 - 20 + t) * 128:(b0 - 20 + t + 1) * 128, :],
                       lambda t, b0=b0: ('xsc', b0 + t))
        if dbg is None or dbg[1] >= 7:
            final_norm(1, 16, lambda t: xssc.ap()[:, :], lambda t: o_ys.ap()[:, :], lambda t: 'xssc')

        rings = {'sp': 24, 'pool': 24}
        S_rings = {q: list(range(n_)) for q, n_ in rings.items()}
        S.finalize(S_rings)
        semh = {}
        for e_ in COMPUTE:
            semh[('eng', e_)] = es.enter_context(nc.semaphore("s_" + e_))
        for q, n_ in rings.items():
            for i in range(n_):
                semh[('ring', q, i)] = es.enter_context(nc.semaphore("r_%s%d" % (q, i)))
        with nc.Block() as block:
            @block.tensor
            def _(e):
                S.emit_engine('pe', e, semh)

            @block.scalar
            def _(e):
                S.emit_engine('act', e, semh)

            @block.vector
            def _(e):
                S.emit_engine('dve', e, semh)

            @block.gpsimd
            def _(e):
                S.emit_engine('pool', e, semh)

            @block.sync
            def _(e):
                S.emit_engine('sp', e, semh)
    return nc, len(S.ops)


def _consts(core):
    I = np.eye(128, dtype=np.float32)
    J = np.ascontiguousarray(I[::-1])
    J16 = np.ascontiguousarray(np.eye(16, dtype=np.float32)[::-1])
    slopes = np.exp2(-8.0 * np.arange(1, 9) / 8).astype(np.float32)
    s = np.arange(128)[:, None]
    q = np.arange(128)[None, :]
    cA = np.zeros((128, 8, 2, 128), np.float32)
    for j in range(2):
        dist = np.abs(128 * (1 - j) + q - s).astype(np.float32)
        ck = 2 * (j - 1) + s // 64
        cq = q // 64
        valid = (ck <= cq) & (ck >= cq - 2)
        for h in range(8):
            cA[:, h, j, :] = np.where(valid, -slopes[h] * dist, NEGM)
    cAs = np.zeros((128, 2, 8, 16), np.float32)
    q16 = np.arange(16)[None, :]
    for j in range(2):
        posk = (896 + s) if j == 0 else (1024 + s)
        dist = np.abs(1024 + q16 - posk).astype(np.float32)
        for h in range(8):
            cAs[:, j, h, :] = -slopes[h] * dist
    cMB = np.zeros((128, 2, 128), np.float32)
    cMB[:, 0, :] = np.where((s < 64) & (q >= 64), NEGM, 0.0)
    cMB[:, 1, :] = np.where((s >= 64) & (q < 64), NEGM, 0.0)
    vb = ((2048 * core - HALO + 128 * np.arange(NBLK)) >= 0).astype(np.float32)
    vblk = np.ascontiguousarray(np.broadcast_to(vb[None, :], (128, NBLK)))
    nv_ = np.zeros(NVON + 3, np.float32)
    nv_[:NVON] = (1.0 - vb[:NVON]) * NEGM
    negrow = np.repeat(nv_[::-1].copy(), 128)[None, :]
    return dict(c_negrow=np.ascontiguousarray(negrow), c_I=I, c_J=J, c_J16=J16, c_A=cA.reshape(128, -1), c_As=cAs.reshape(128, -1), c_MB=np.ascontiguousarray(cMB[::-1]).reshape(128, -1), c_vblk=vblk)


_NC = None
_DBG_HOOK = None


def kernel(x_prompt, x_sample, mem_prompt, cache_a_k, cache_a_v, cache_b_k, cache_b_v,
           cache_mem_k, cache_mem_v, state_conv, g_mix, w_mix_in, a_sink, b_rel_bias,
           w_o_a, w_o_b, w_mix_out, g_xattn, g_mem, w_xq, w_xk, w_xv, w_xo,
           g_ffn, w_up, w_conv, b_conv, w_down, g_final):
    global _NC
    f = lambda a: np.ascontiguousarray(np.asarray(a, dtype=np.float32))
    if _NC is None:
        _NC = build()[0]
    nc = _NC
    xp = f(x_prompt)[0]
    xpad = np.concatenate([np.zeros((HALO, D), np.float32), xp], axis=0)
    sink = f(a_sink)
    sink_l = np.zeros((128, 16), np.float32)
    for l in range(DEPTH):
        for j in range(4):
            sink_l[0:64, l * 4 + j] = sink[l, 2 * j]
            sink_l[64:128, l * 4 + j] = sink[l, 2 * j + 1]
    wc = f(w_conv).reshape(DEPTH, 3, 44, 128).transpose(3, 0, 1, 2).reshape(128, -1)
    bc = f(b_conv).reshape(DEPTH, 44, 128).transpose(2, 0, 1).reshape(128, -1)
    shared = dict(mem=f(mem_prompt)[0], g_mix=f(g_mix), g_xattn=f(g_xattn), g_mem=f(g_mem), g_ffn=f(g_ffn),
                  g_final=f(g_final).reshape(1, D), w_mix_in=f(w_mix_in), sink_l=sink_l, b_rel_bias=f(b_rel_bias), rel_rev=np.ascontiguousarray(f(b_rel_bias)[:, :, ::-1]),
                  w_o_a=f(w_o_a), w_o_b=f(w_o_b), w_mix_out=f(w_mix_out), w_xq=f(w_xq), w_xk=f(w_xk), w_xv=f(w_xv),
                  w_xo=f(w_xo), w_up=f(w_up), w_down=f(w_down), wconv_l=np.ascontiguousarray(wc), bconv_l=np.ascontiguousarray(bc))
    in_maps = []
    for c in range(8):
        m = dict(shared)
        m.update(_consts(c))
        m['xin'] = np.ascontiguousarray(xpad[2048 * c:2048 * c + LTOK])
        m['xs'] = f(x_sample)[c]
        m['cak'] = f(cache_a_k)[:, c].reshape(DEPTH, 128, 128)
        m['cav'] = f(cache_a_v)[:, c].reshape(DEPTH, 128, 128)
        m['cbk'] = f(cache_b_k)[:, c].reshape(DEPTH, 512, 512)
        m['cbv'] = f(cache_b_v)[:, c].reshape(DEPTH, 512, 512)
        m['cmk'] = f(cache_mem_k)[:, c].reshape(DEPTH, 256, 512)
        m['cmv'] = f(cache_mem_v)[:, c].reshape(DEPTH, 256, 512)
        m['sconv_in'] = np.ascontiguousarray(f(state_conv)[:, c].reshape(DEPTH, 2, 44, 128).transpose(0, 3, 2, 1))
        in_maps.append({k: np.ascontiguousarray(v) for k, v in m.items()})
    if _DBG_HOOK is not None:
        return _DBG_HOOK(in_maps)
    res = run_bass_kernel_spmd(nc, in_maps, core_ids=list(range(8))).results
    R = [{k: np.asarray(v, dtype=np.float32) for k, v in r.items()} for r in res]
    y_prompt = np.concatenate([R[c]['o_y'] for c in range(8)], axis=0)[None]
    y_sample = np.stack([R[c]['o_ys'] for c in range(8)], axis=0)
    pkv = R[7]['o_pkv']
    pa_k = pkv[:, 3, :, 0:128].reshape(DEPTH, 1, 128, 2, 64)
    pa_v = pkv[:, 3, :, 128:256].reshape(DEPTH, 1, 128, 2, 64)
    pb_k = pkv[:, :, :, 256:768].reshape(DEPTH, 1, 512, 8, 64)
    pb_v = pkv[:, :, :, 768:1280].reshape(DEPTH, 1, 512, 8, 64)
    pm = R[0]['o_pm']
    pm_k = pm[:, :, 0:512].reshape(DEPTH, 1, 256, 4, 128)
    pm_v = pm[:, :, 512:1024].reshape(DEPTH, 1, 256, 4, 128)
    cv = lambda a: a.transpose(0, 3, 2, 1).reshape(DEPTH, 2, 5632)
    pconv = cv(R[7]['o_pconv'])[:, None]
    skv = np.stack([R[c]['o_skv'] for c in range(8)], axis=1)
    sa_k = skv[..., 0:128].reshape(DEPTH, 8, 16, 2, 64)
    sa_v = skv[..., 128:256].reshape(DEPTH, 8, 16, 2, 64)
    sb_k = skv[..., 256:768].reshape(DEPTH, 8, 16, 8, 64)
    sb_v = skv[..., 768:1280].reshape(DEPTH, 8, 16, 8, 64)
    sconv = np.stack([cv(R[c]['o_sconv']) for c in range(8)], axis=1)
    c32 = lambda a: np.ascontiguousarray(a, dtype=np.float32)
    return tuple(c32(a) for a in (y_prompt, y_sample, pa_k, pa_v, pb_k, pb_v, pm_k, pm_v, pconv, sa_k, sa_v, sb_k, sb_v, sconv))
```
